# Optimizing a Trainium2 kernel written in Bass

```python
import math
import jax, jax.numpy as jnp
from jax import lax
import numpy as np

D_MODEL = 1024
BATCH = 4
SEQ = 4096
DEPTH = 1

CHUNK = 64
LEFT_CHUNKS = 8
BAND = LEFT_CHUNKS + 1
N_HEADS = 16
HEAD_DIM = 64
D_ATTN = N_HEADS * HEAD_DIM
MAX_REL = 128
D_RNN = D_MODEL
N_RG_BLOCKS = 16
RG_BLOCK = D_RNN // N_RG_BLOCKS
CONV_W = 4
RG_C = 8.0
D_FF = ((8 * D_MODEL // 3 + 255) // 256) * 256
IN_COLS = 3 * D_ATTN + 2 * D_RNN + 2 * D_MODEL
EPS = 1e-6
NEG_INF = -1e30

kernel_name = "hybrid_chunked_attn_rglru_gated_block"


def rmsnorm(x, g):
    xf = x.astype(jnp.float32)
    var = jnp.mean(xf * xf, axis=-1, keepdims=True)
    return (xf * lax.rsqrt(var + EPS) * g.astype(jnp.float32)).astype(x.dtype)


def rel_bias_band(table):
    i = jnp.arange(CHUNK)[:, None]
    m = jnp.arange(BAND * CHUNK)[None, :]
    dist = i - m + LEFT_CHUNKS * CHUNK
    idx = jnp.clip(dist, -MAX_REL, MAX_REL) + MAX_REL
    return table[:, idx]


def chunked_band_attention(q, k, v, rel_table):
    B, S, _ = q.shape
    nc = S // CHUNK
    q = q.reshape(B, nc, CHUNK, N_HEADS, HEAD_DIM)
    k = k.reshape(B, nc, CHUNK, N_HEADS, HEAD_DIM)
    v = v.reshape(B, nc, CHUNK, N_HEADS, HEAD_DIM)
    pad = jnp.zeros((B, LEFT_CHUNKS, CHUNK, N_HEADS, HEAD_DIM), k.dtype)
    kp = jnp.concatenate([pad, k], axis=1)
    vp = jnp.concatenate([pad, v], axis=1)
    kb = jnp.concatenate([kp[:, j:j + nc] for j in range(BAND)], axis=2)
    vb = jnp.concatenate([vp[:, j:j + nc] for j in range(BAND)], axis=2)
    scale = 1.0 / math.sqrt(HEAD_DIM)
    s = jnp.einsum('bnqhd,bnkhd->bhnqk', q, kb).astype(jnp.float32) * scale
    s = s + rel_bias_band(rel_table).astype(jnp.float32)[None, :, None, :, :]
    key_chunk = jnp.arange(nc)[:, None] - LEFT_CHUNKS + (jnp.arange(BAND * CHUNK)[None, :] // CHUNK)
    valid = key_chunk >= 0
    s = jnp.where(valid[None, None, :, None, :], s, NEG_INF)
    p = jax.nn.softmax(s, axis=-1).astype(v.dtype)
    o = jnp.einsum('bhnqk,bnkhd->bnqhd', p, vb)
    return o.reshape(B, S, D_ATTN)


def causal_depthwise_conv(x, w, b):
    S = x.shape[1]
    xp = jnp.pad(x, ((0, 0), (CONV_W - 1, 0), (0, 0)))
    y = sum(xp[:, j:j + S] * w[j] for j in range(CONV_W))
    return y + b


def rg_lru(x, w_r, b_r, w_i, b_i, lam):
    B, S, D = x.shape
    xb = x.reshape(B, S, N_RG_BLOCKS, RG_BLOCK)
    r = jax.nn.sigmoid(jnp.einsum('bsnd,nde->bsne', xb, w_r).reshape(B, S, D) + b_r)
    i = jax.nn.sigmoid(jnp.einsum('bsnd,nde->bsne', xb, w_i).reshape(B, S, D) + b_i)
    log_a = -RG_C * r.astype(jnp.float32) * jax.nn.softplus(-lam.astype(jnp.float32))
    a = jnp.exp(log_a)
    mult = jnp.sqrt(-jnp.expm1(2.0 * log_a))
    bx = mult * (i * x).astype(jnp.float32)

    def combine(left, right):
        a1, b1 = left
        a2, b2 = right
        return a1 * a2, a2 * b1 + b2

    _, h = lax.associative_scan(combine, (a, bx), axis=1)
    return h.astype(x.dtype)


def setup_inputs(seed: int = 0) -> dict:
    key = jax.random.key(seed)
    ks = jax.random.split(key, 20)
    L = DEPTH

    def nrm(k, shape, fan_in):
        return jax.random.normal(k, shape, jnp.float32) * (fan_in ** -0.5)

    x = jax.random.normal(ks[0], (BATCH, SEQ, D_MODEL), jnp.float32)
    norm_mix_g = 1.0 + 0.05 * jax.random.normal(ks[1], (L, D_MODEL), jnp.float32)
    w_in = nrm(ks[2], (L, D_MODEL, IN_COLS), D_MODEL)
    b_merge = 0.01 * jax.random.normal(ks[3], (L, 2 * D_MODEL), jnp.float32)
    rel_table = 0.5 * jax.random.normal(ks[4], (L, N_HEADS, 2 * MAX_REL + 1), jnp.float32)
    w_attn_out = nrm(ks[5], (L, D_ATTN, D_MODEL), D_ATTN)
    conv_w = nrm(ks[6], (L, CONV_W, D_RNN), CONV_W)
    conv_b = 0.01 * jax.random.normal(ks[7], (L, D_RNN), jnp.float32)
    w_rg_r = nrm(ks[8], (L, N_RG_BLOCKS, RG_BLOCK, RG_BLOCK), RG_BLOCK)
    b_rg_r = 0.01 * jax.random.normal(ks[9], (L, D_RNN), jnp.float32)
    w_rg_i = nrm(ks[10], (L, N_RG_BLOCKS, RG_BLOCK, RG_BLOCK), RG_BLOCK)
    b_rg_i = 0.01 * jax.random.normal(ks[11], (L, D_RNN), jnp.float32)
    u = jax.random.uniform(ks[12], (L, D_RNN), jnp.float32, 0.9, 0.999)
    a0 = u ** (1.0 / RG_C)
    rg_lambda = jnp.log(a0) - jnp.log1p(-a0)
    w_rnn_out = nrm(ks[13], (L, D_RNN, D_MODEL), D_RNN)
    w_o = nrm(ks[14], (L, D_MODEL, D_MODEL), D_MODEL)
    norm_ffn_g = 1.0 + 0.05 * jax.random.normal(ks[15], (L, D_MODEL), jnp.float32)
    w_ffn_in = nrm(ks[16], (L, D_MODEL, 2 * D_FF), D_MODEL)
    w_ffn_out = nrm(ks[17], (L, D_FF, D_MODEL), D_FF)
    final_norm_g = 1.0 + 0.05 * jax.random.normal(ks[18], (D_MODEL,), jnp.float32)
    return {"x": x, "norm_mix_g": norm_mix_g, "w_in": w_in, "b_merge": b_merge,
            "rel_table": rel_table, "w_attn_out": w_attn_out, "conv_w": conv_w,
            "conv_b": conv_b, "w_rg_r": w_rg_r, "b_rg_r": b_rg_r, "w_rg_i": w_rg_i,
            "b_rg_i": b_rg_i, "rg_lambda": rg_lambda, "w_rnn_out": w_rnn_out, "w_o": w_o,
            "norm_ffn_g": norm_ffn_g, "w_ffn_in": w_ffn_in, "w_ffn_out": w_ffn_out,
            "final_norm_g": final_norm_g}


def reference(x, norm_mix_g, w_in, b_merge, rel_table, w_attn_out, conv_w, conv_b,
              w_rg_r, b_rg_r, w_rg_i, b_rg_i, rg_lambda, w_rnn_out, w_o,
              norm_ffn_g, w_ffn_in, w_ffn_out, final_norm_g):
    h = x
    splits = [D_ATTN, 2 * D_ATTN, 3 * D_ATTN, 3 * D_ATTN + D_RNN,
              3 * D_ATTN + 2 * D_RNN, 3 * D_ATTN + 2 * D_RNN + D_MODEL]
    for l in range(DEPTH):
        xn = rmsnorm(h, norm_mix_g[l])
        u = jnp.einsum('bsd,de->bse', xn, w_in[l])
        q, k, v, xr, gr, ga, gb = jnp.split(u, splits, axis=-1)
        ya = chunked_band_attention(q, k, v, rel_table[l]) @ w_attn_out[l]
        xc = causal_depthwise_conv(xr, conv_w[l], conv_b[l])
        hr = rg_lru(xc, w_rg_r[l], b_rg_r[l], w_rg_i[l], b_rg_i[l], rg_lambda[l])
        yb = (hr * jax.nn.gelu(gr)) @ w_rnn_out[l]
        gates = jax.nn.sigmoid(jnp.concatenate([ga, gb], axis=-1) + b_merge[l])
        g_a, g_b = jnp.split(gates, 2, axis=-1)
        h = h + (g_a * ya + g_b * yb) @ w_o[l]
        hn = rmsnorm(h, norm_ffn_g[l])
        gu = hn @ w_ffn_in[l]
        g, up = jnp.split(gu, 2, axis=-1)
        h = h + (jax.nn.silu(g) * up) @ w_ffn_out[l]
    return rmsnorm(h, final_norm_g)
```

```python
import contextlib
import numpy as np
import concourse.bass as bass
import concourse.mybir as mybir
from concourse.bass_utils import run_bass_kernel_spmd

F32 = mybir.dt.float32
BF16 = mybir.dt.bfloat16
AF = mybir.ActivationFunctionType
ALU = mybir.AluOpType

PE, ACT, DVE, POOL, SP = "tensor", "scalar", "vector", "gpsimd", "sync"
ENGINES = [PE, ACT, DVE, POOL, SP]
NDMA_SEM = 12

D = 1024
NH = 16
DFF = 2816
TM = 2048
TPRE = 2048
HALO = 512
NEG = -1e30
FGROUPS = [(0, 8), (8, 15), (15, 22)]

CW, CB, BR, BI, LAM, BMA, BMB, NV = 0, 32, 40, 48, 56, 64, 72, 80


class Res:
    __slots__ = ("name", "w", "r")

    def __init__(self, name=""):
        self.name = name
        self.w = None
        self.r = []


class Op:
    __slots__ = ("eng", "fn", "reads", "writes", "kind", "deps", "signal",
                 "tick", "sem", "gidx")

    def __init__(self, eng, fn, reads, writes, kind):
        self.eng = eng
        self.fn = fn
        self.reads = reads
        self.writes = writes
        self.kind = kind
        self.deps = []
        self.signal = False
        self.tick = None
        self.sem = None


class Prog:
    def __init__(self, nc):
        self.nc = nc
        self.ops = []
        self.res = {}

    def R(self, name):
        r = self.res.get(name)
        if r is None:
            r = self.res[name] = Res(name)
        return r

    def _rs(self, lst):
        return [self.R(x) if isinstance(x, str) else x for x in lst]

    def op(self, eng, fn, reads=(), writes=()):
        o = Op(eng, fn, self._rs(reads), self._rs(writes), "op")
        self._add(o)
        return o

    def dma(self, eng, out, in_, reads=(), writes=()):
        o = Op(eng, (out, in_), self._rs(reads), self._rs(writes), "dma")
        self._add(o)
        return o

    def barrier(self):
        lb = getattr(self, "_last_barrier", -1)
        last = {}
        dmas = {SP: [], POOL: []}
        for o in self.ops:
            if o.kind == "dma":
                dmas[o.eng].append(o)
            elif o.kind == "op" and o.gidx > lb:
                last[o.eng] = o
        deps = list(last.values()) + dmas[SP][-NDMA_SEM:] + dmas[POOL][-NDMA_SEM:]
        for e in ENGINES:
            o = Op(e, (lambda eng: eng.nop(nofuse=True)), [], [], "nop")
            o.gidx = len(self.ops)
            o.deps = [d for d in deps if d.kind == "dma" or d.eng != e]
            for d in o.deps:
                d.signal = True
            self.ops.append(o)
        self._last_barrier = len(self.ops) - 1
        for r in self.res.values():
            r.w = None
            r.r = []

    def _add(self, o):
        o.gidx = len(self.ops)
        deps = {}
        for r in o.reads:
            if r.w is not None:
                deps[r.w.gidx] = r.w
        for w in o.writes:
            if w.w is not None:
                deps[w.w.gidx] = w.w
            lastr = {}
            for rd in w.r:
                if rd.kind == "dma":
                    deps[rd.gidx] = rd
                else:
                    lastr[rd.eng] = rd
            for rd in lastr.values():
                deps[rd.gidx] = rd
        dl = []
        for d in deps.values():
            if d is o:
                continue
            if d.kind != "dma" and o.kind != "dma" and d.eng == o.eng and o.eng == PE:
                continue
            dl.append(d)
        o.deps = dl
        for d in dl:
            d.signal = True
        for r in o.reads:
            r.r.append(o)
        for w in o.writes:
            w.w = o
            w.r = []
        self.ops.append(o)

    def emit(self, final_waits=()):
        nc = self.nc
        for o in final_waits:
            o.signal = True
        with contextlib.ExitStack() as es:
            esem = {e: es.enter_context(nc.semaphore("s_" + e)) for e in ENGINES}
            dsem = {e: [es.enter_context(nc.semaphore("d_%s_%d" % (e, i)))
                        for i in range(NDMA_SEM)] for e in (SP, POOL)}
            cnt = {e: 0 for e in ENGINES}
            dcnt = {e: 0 for e in (SP, POOL)}
            per_eng = {e: [] for e in ENGINES}
            for o in self.ops:
                if o.kind == "dma":
                    n = dcnt[o.eng]
                    dcnt[o.eng] += 1
                    o.sem = dsem[o.eng][n % NDMA_SEM]
                    o.tick = 16 * (n // NDMA_SEM + 1)
                elif o.signal:
                    cnt[o.eng] += 1
                    o.tick = cnt[o.eng]
                    o.sem = esem[o.eng]
                per_eng[o.eng].append(o)
            self.stats = {e: len(per_eng[e]) for e in ENGINES}
            self.stats["ticks"] = dict(cnt)
            self.stats["dmas"] = dict(dcnt)
            block = es.enter_context(nc.Block())

            def make(e):
                ops = per_eng[e]

                def body(eng):
                    seen = {}
                    ring_last = {}

                    def wait(sem, val):
                        k = id(sem)
                        if seen.get(k, 0) >= val:
                            return
                        eng.wait_ge(sem, val)
                        seen[k] = val

                    for o in ops:
                        for d in o.deps:
                            wait(d.sem, d.tick)
                        if o.kind == "dma":
                            k = id(o.sem)
                            if k in ring_last:
                                wait(o.sem, ring_last[k])
                            ring_last[k] = o.tick
                            out, in_ = o.fn
                            eng.dma_start(out=out, in_=in_).then_inc(o.sem, 16)
                        else:
                            ins = o.fn(eng)
                            if o.signal:
                                ins.then_inc(o.sem, 1)
                    if e == SP:
                        for o in final_waits:
                            wait(o.sem, o.tick)
                return body

            for e in ENGINES:
                getattr(block, e)(make(e))


def build_program(debug=False):
    nc = bass.Bass("TRN2", target_bir_lowering=False)

    def din(name, shape):
        return nc.dram_tensor(name, shape, F32, kind="ExternalInput").ap()

    xm = din("xm", [TM, D])
    xp = din("xp", [TPRE, D])
    w_in = din("w_in", [D, 7 * D])
    w_ao = din("w_ao", [D, D])
    w_ro = din("w_ro", [D, D])
    w_o = din("w_o", [D, D])
    w_fi = din("w_fi", [D, 2 * DFF])
    w_fo = din("w_fo", [DFF, D])
    wr_bd = din("wr_bd", [128, 8, 128])
    wi_bd = din("wi_bd", [128, 8, 128])
    cvec = din("cvec", [128, NV])
    gains = din("gains", [3, D])
    biasT = din("biasT", [8, 128, 2 * 5 * 128])
    pv = din("pv", [128, 2])
    cbrow_d = din("cbrow", [1, D])
    out = nc.dram_tensor("out", [TM, D], F32, kind="ExternalOutput").ap()

    KB = 1024
    with contextlib.ExitStack() as es:
        arena = es.enter_context(nc.sbuf_tensor("arena", [128, 94 * KB], BF16))
        cpool = es.enter_context(nc.sbuf_tensor("cpool", [128, 1024], F32))
        identb = es.enter_context(nc.sbuf_tensor("identb", [128, 128], BF16))
        wbd = es.enter_context(nc.sbuf_tensor("wbd", [128, 2, 8, 128], BF16))
        dg = es.enter_context(nc.sbuf_tensor("dg", [128, 8, 4, 128], BF16))
        cbrow = es.enter_context(nc.sbuf_tensor("cbrow_sb", [1, D], BF16))
        onesr = es.enter_context(nc.sbuf_tensor("onesr", [1, 512], BF16))
        psum = es.enter_context(nc.psum_tensor("psum", [128, 4096], F32))
        pg = [psum[:, i * 512:(i + 1) * 512] for i in range(8)]
        pg_avail = [list(range(8))]

        def view(off, shape, dt):
            n = 1
            for s in shape[1:]:
                n *= s
            esz = 2 if dt == BF16 else 4
            a = arena[:, off // 2: off // 2 + n * esz // 2]
            if dt != BF16:
                a = a.bitcast(dt)
            if len(shape) == 3:
                a = a.rearrange("p (a b) -> p a b", a=shape[1])
            elif len(shape) == 4:
                a = a.rearrange("p (a b c) -> p a b c", a=shape[1], b=shape[2])
            return a

        cv = cpool[:, 0:NV]
        dv = cpool[:, 88:88 + 40]
        identf = cpool[:, 768:896]
        pvt = cpool[:, 128:130]
        cst = cpool[:, 132:136]
        ss = cpool[:, 160:160 + 32]
        rs = cpool[:, 192:192 + 32]
        hist = cpool[:, 224:224 + 24].rearrange("p (c j) -> p c j", c=8)
        hlast = cpool[:, 256:264]
        hinit = cpool[:, 264:272]
        iot = cpool[:, 512:640]
        osum = cpool[:, 640:644]

        P = Prog(nc)
        pgi = [0]

        def next_pg():
            av = pg_avail[0]
            i = av[pgi[0] % len(av)]
            pgi[0] += 1
            return pg[i], "pg%d" % i

        def pg_bf(ps):
            return ps.bitcast(BF16).rearrange("p (a b) -> p a b", a=8)

        def wslice(src, c0, ncols):
            return src[:, c0:c0 + ncols].rearrange("(k p) n -> p k n", p=128)

        def branch_pre(wsrc, gate_c0, WOFF):
            P.dma(POOL, view(WOFF, [128, 8, 128], BF16), wslice(wsrc, 0, 128), writes=["WA0"])
            P.dma(POOL, view(WOFF + 4096, [128, 8, 128], BF16), wslice(w_in, gate_c0, 128), writes=["WG0"])

        P.dma(SP, cv, cvec, writes=["cv"])
        P.dma(SP, pvt, pv, writes=["pv"])
        P.dma(POOL, wbd[:, 0], wr_bd, writes=["wbd"])
        P.dma(POOL, wbd[:, 1], wi_bd, writes=["wbd"])
        P.op(POOL, lambda e: e.iota(iot, [[1, 128]], base=0, channel_multiplier=-1,
                                    allow_small_or_imprecise_dtypes=True), writes=["iot"])
        P.op(DVE, lambda e: e.tensor_single_scalar(identb[:], iot, 0.0, ALU.is_equal),
             reads=["iot"], writes=["ident"])
        P.op(DVE, lambda e: e.memset(cst[:, 0:1], 1.0), writes=["cst"])
        P.op(DVE, lambda e: e.memset(cst[:, 1:2], 1e-6), writes=["cst"])
        P.op(DVE, lambda e: e.memset(cst[:, 2:3], 0.0), writes=["cst"])
        P.op(DVE, lambda e: e.memset(ss, 0.0), writes=["ss%d" % i for i in range(32)])
        P.op(DVE, lambda e: e.tensor_scalar_mul(dv[:, 0:8], cv[:, BR:BR + 8], 0.5), reads=["cv"], writes=["dv"])
        P.op(DVE, lambda e: e.tensor_scalar_mul(dv[:, 8:16], cv[:, BI:BI + 8], 0.5), reads=["cv"], writes=["dv"])
        P.op(ACT, lambda e: e.activation(dv[:, 32:40], cv[:, LAM:LAM + 8], AF.Exp, scale=-1.0),
             reads=["cv"], writes=["dvt"])
        P.op(ACT, lambda e: e.activation(dv[:, 32:40], dv[:, 32:40], AF.Ln, bias=cst[:, 0:1]),
             reads=["dvt", "cst"], writes=["dvt"])
        P.op(DVE, lambda e: e.tensor_scalar_mul(dv[:, 16:24], dv[:, 32:40], -4.0), reads=["dvt"], writes=["dv"])
        P.op(DVE, lambda e: e.tensor_scalar_mul(dv[:, 24:32], dv[:, 32:40], -2.0), reads=["dvt"], writes=["dv"])
        P.op(DVE, lambda e: e.tensor_single_scalar(identf, iot, 0.0, ALU.is_equal), reads=["iot"], writes=["identf"])
        def dg_build(c, j):
            P.op(DVE, lambda e, c=c, j=j: e.tensor_scalar_mul(dg[:, c, j, :], identf,
                                                             cv[:, CW + 8 * j + c:CW + 8 * j + c + 1]),
                 reads=["identf", "cv"], writes=["dg"])
        P.dma(POOL, cbrow[:], cbrow_d, writes=["cbrow"])
        P.op(DVE, lambda e: e.memset(onesr[:], 1.0), writes=["onesr"])

        HBR, HBI, HC, QC = 0, 8, 16, 24

        def norm_a(src_ap, src_res, gt, si, xnb, xn_res):
            junk = xnb
            P.op(ACT, lambda e: e.activation(junk, src_ap, AF.Square, accum_out=ss[:, si:si + 1]),
                 reads=[src_res], writes=[xn_res, "ss%d" % si])
            P.op(ACT, lambda e: e.activation(rs[:, si:si + 1], ss[:, si:si + 1], AF.Sqrt,
                                             scale=1.0 / D, bias=cst[:, 1:2]),
                 reads=["ss%d" % si, "cst"], writes=["rs%d" % si])
            P.op(DVE, lambda e: e.reciprocal(rs[:, si:si + 1], rs[:, si:si + 1]),
                 reads=["rs%d" % si], writes=["rs%d" % si])
            P.op(DVE, lambda e: e.scalar_tensor_tensor(xnb, src_ap, rs[:, si:si + 1], gt, ALU.mult, ALU.mult),
                 reads=[src_res, "rs%d" % si, "g"], writes=[xn_res])

        def norm_s(src_ap, src_res, si, junk):
            P.op(ACT, lambda e: e.activation(junk, src_ap, AF.Square, accum_out=ss[:, si:si + 1]),
                 reads=[src_res], writes=["ss%d" % si])
            P.op(ACT, lambda e: e.activation(rs[:, si:si + 1], ss[:, si:si + 1], AF.Sqrt,
                                             scale=1.0 / D, bias=cst[:, 1:2]),
                 reads=["ss%d" % si, "cst"], writes=["rs%d" % si])

        def norm_m(src_ap, src_res, gt, si, xnb, xn_res):
            P.op(DVE, lambda e: e.reciprocal(rs[:, si:si + 1], rs[:, si:si + 1]),
                 reads=["rs%d" % si], writes=["rs%d" % si])
            P.op(DVE, lambda e: e.scalar_tensor_tensor(xnb, src_ap, rs[:, si:si + 1], gt, ALU.mult, ALU.mult),
                 reads=[src_res, "rs%d" % si, "g"], writes=[xn_res])

        def norm_b(dst3, dst_res, xnb, xn_res):
            ps, pr = next_pg()
            pv3 = pg_bf(ps)
            for k in range(8):
                P.op(PE, lambda e, k=k: e.transpose(pv3[:, k, :], xnb[:, k * 128:(k + 1) * 128], identb[:]),
                     reads=[xn_res, "ident"], writes=[pr])
            P.op(ACT, lambda e: e.copy(dst3, pv3), reads=[pr], writes=[dst_res])

        XT = view(0, [128, 8, 2560], BF16)
        XTp = view(152 * KB, [128, 8, 1536], BF16)
        xt_p = [view(84 * KB + i * 8192, [128, 2, D], F32) for i in range(3)]
        xt_b = [xt_p[i // 2][:, i % 2, :] for i in range(6)]
        xn_b = [view(108 * KB + i * 2048, [128, D], BF16) for i in range(3)]
        gt = view(114 * KB, [128, D], F32)
        P.dma(SP, gt, gains[0:1, :].broadcast_to([128, D]), writes=["g"])

        ORDER = list(range(12, 32)) + list(range(0, 12))

        def p0_load(p):
            if p % 2:
                return
            t = ORDER[p]
            src = xp[t * 128:(t + 2) * 128, :] if t < 16 else xm[(t - 16) * 128:(t - 14) * 128, :]
            P.dma(SP, xt_p[(p % 6) // 2], src.rearrange("(a p) d -> p a d", p=128),
                  writes=["xt%d" % (p % 6), "xt%d" % (p % 6 + 1)])

        V = view(40 * KB, [128, 20, 16, 65], BF16)
        WV2 = [view(118 * KB + i * 8192, [128, 8, 512], BF16) for i in range(2)]
        for cc in range(2):
            P.dma(POOL, WV2[cc], wslice(w_in, 2 * D + cc * 512, 512), writes=["WV%d" % cc])
        P.op(DVE, lambda e: e.memset(V[:, :, :, 64:65], 1.0), writes=["Vones"])

        def v_group(t, cc):
            ps, pr = next_pg()
            for k in range(8):
                P.op(PE, lambda e, ps=ps, k=k, t=t, cc=cc: e.matmul(ps, XT[:, k, t * 128:(t + 1) * 128], WV2[cc][:, k, :],
                                                                start=(k == 0), stop=(k == 7)),
                     reads=["WV%d" % cc, "XT%d" % t], writes=[pr])
            P.op(DVE, lambda e, ps=ps, t=t, cc=cc: e.tensor_copy(V[:, t, cc * 8:(cc + 1) * 8, 0:64],
                                                             ps.rearrange("p (h d) -> p h d", h=8)),
                 reads=[pr], writes=["V%d_%d" % (t, cc)])

        def p0_dst(t):
            if t < 12:
                return XTp[:, :, t * 128:(t + 1) * 128], "XTp%d" % t
            return XT[:, :, (t - 12) * 128:(t - 11) * 128], "XT%d" % (t - 12)

        junk0 = view(134 * KB, [128, D], BF16)
        vsched = {}
        for g in range(40):
            vsched.setdefault(1 + (g * 31) // 40, []).append(g)
        for p in range(4):
            p0_load(p)
        norm_s(xt_b[0], "xt0", ORDER[0], junk0)
        norm_s(xt_b[1], "xt1", ORDER[1], junk0)
        norm_m(xt_b[0], "xt0", gt, ORDER[0], xn_b[0], "xn0")
        for p in range(32):
            if p + 4 < 32:
                p0_load(p + 4)
            if p + 2 < 32:
                norm_s(xt_b[(p + 2) % 6], "xt%d" % ((p + 2) % 6), ORDER[p + 2], junk0)
            if p + 1 < 32:
                norm_m(xt_b[(p + 1) % 6], "xt%d" % ((p + 1) % 6), gt, ORDER[p + 1], xn_b[(p + 1) % 3],
                       "xn%d" % ((p + 1) % 3))
            dst, dres = p0_dst(ORDER[p])
            norm_b(dst, dres, xn_b[p % 3], "xn%d" % (p % 3))
            for g in vsched.get(p, []):
                v_group(g // 2, g % 2)
            dg_build(p // 4, p % 4)
        for g in vsched.get(32, []):
            v_group(g // 2, g % 2)
        P.dma(POOL, view(176 * KB, [128, 8, 128], BF16), wslice(w_in, 3 * D, 128), writes=["wx0"])
        P.barrier()

        def rnn_pass(main, T0, WOFF, pre0=False):
            o = T0
            xrb = view(o, [128, 2112], BF16); o += 4224
            xcb = view(o, [128, 2048], BF16); o += 4096
            ab2 = [view(o + i * 8192, [128, 2048], F32) for i in range(2)]; o += 16384
            wb2 = [view(o + i * 8192, [128, 2048], F32) for i in range(2)]; o += 16384
            ixb = view(o, [128, 2048], F32); o += 8192
            bxb = view(o, [128, 2048], F32); o += 8192
            hb2 = [view(o + i * 2048, [128, 512], F32) for i in range(2)]; o += 4096
            trb = [view(o + i * 2048, [128, 512], F32) for i in range(2)]; o += 4096
            tib = [view(o + i * 1024, [128, 512], BF16) for i in range(2)]; o += 2048
            assert o - T0 <= 72 * KB
            wx = [view(WOFF + i * 2048, [128, 8, 128], BF16) for i in range(2)]
            BT = view(144 * KB, [128, 8, 2048], BF16)
            if not main:
                P.op(DVE, lambda e: e.memset(xrb[:, 0:3], 0.0), writes=["xrh"])
            else:
                P.op(DVE, lambda e: e.tensor_scalar_mul(hinit, hlast, pvt[:, 0:1]),
                     reads=["hlast", "pv"], writes=["hinit"])
            NQ = 32
            st = {}

            def cs_(n):
                return slice(n * 512, (n + 1) * 512)

            def sA(q):
                c, n = divmod(q, 4)
                b = c % 2
                if n == 0 and not (pre0 and c == 0):
                    P.dma(POOL, wx[b], wslice(w_in, 3 * D + c * 128, 128), writes=["wx%d" % b])
                ps, pr = next_pg()
                st[("xr", q)] = (ps, pr)
                for k in range(8):
                    if main:
                        rhs = XT[:, k, 512 + n * 512: 512 + (n + 1) * 512]
                    elif n < 3:
                        rhs = XTp[:, k, cs_(n)]
                    else:
                        rhs = XT[:, k, 0:512]
                    P.op(PE, lambda e, ps=ps, k=k, rhs=rhs, b=b: e.matmul(ps, wx[b][:, k, :], rhs,
                                                                     start=(k == 0), stop=(k == 7)),
                         reads=["wx%d" % b], writes=[pr])

            def sB(q):
                c, n = divmod(q, 4)
                ps, pr = st.pop(("xr", q))
                if main and n == 0:
                    P.op(DVE, lambda e, c=c: e.tensor_copy(xrb[:, 0:3], hist[:, c, :]),
                         reads=["hist"], writes=["xrh"])
                P.op(DVE, lambda e, ps=ps, n=n: e.tensor_copy(xrb[:, 3 + n * 512: 3 + (n + 1) * 512], ps),
                     reads=[pr], writes=["xr%d" % n])

            def sC(q):
                c, n = divmod(q, 4)
                ps, pr = next_pg()
                st[("xc", q)] = (ps, pr)
                rd = ["xr%d" % n, "dg"] + (["xr%d" % (n - 1)] if n else ["xrh"])
                for j in range(4):
                    P.op(PE, lambda e, ps=ps, j=j, c=c, n=n: e.matmul(ps, dg[:, c, j, :],
                                                                  xrb[:, n * 512 + j: n * 512 + j + 512],
                                                                  start=(j == 0), stop=False),
                         reads=rd, writes=[pr])
                P.op(PE, lambda e, ps=ps, c=c: e.matmul(ps, cbrow[0:1, c * 128:(c + 1) * 128], onesr[0:1, :],
                                                      start=False, stop=True),
                     reads=["cbrow", "onesr"], writes=[pr])

            def sD(q):
                c, n = divmod(q, 4)
                ps, pr = st.pop(("xc", q))
                P.op(DVE, lambda e, ps=ps, n=n: e.tensor_copy(xcb[:, cs_(n)], ps), reads=[pr], writes=["xc%d" % n])

            def sE(q):
                c, n = divmod(q, 4)
                psr, prr = next_pg()
                P.op(PE, lambda e, psr=psr, c=c, n=n: e.matmul(psr, wbd[:, 0, c, :], xcb[:, cs_(n)], start=True, stop=True),
                     reads=["wbd", "xc%d" % n], writes=[prr])
                psi, pri = next_pg()
                P.op(PE, lambda e, psi=psi, c=c, n=n: e.matmul(psi, wbd[:, 1, c, :], xcb[:, cs_(n)], start=True, stop=True),
                     reads=["wbd", "xc%d" % n], writes=[pri])
                st[("g", q)] = (psr, prr, psi, pri)

            def sF(q):
                c, n = divmod(q, 4)
                psr, prr, psi, pri = st.pop(("g", q))
                tr, ti, tn = trb[q % 2], tib[q % 2], q % 2
                ab, wb, cp = ab2[c % 2], wb2[c % 2], c % 2
                P.op(ACT, lambda e, psr=psr, c=c, tr=tr: e.activation(tr, psr, AF.Tanh, scale=0.5,
                                                                  bias=dv[:, HBR + c:HBR + c + 1]),
                     reads=[prr, "dv"], writes=["tr%d" % tn])
                P.op(ACT, lambda e, psi=psi, c=c, ti=ti: e.activation(ti, psi, AF.Tanh, scale=0.5,
                                                                  bias=dv[:, HBI + c:HBI + c + 1]),
                     reads=[pri, "dv"], writes=["ti%d" % tn])
                P.op(ACT, lambda e, c=c, n=n, tr=tr, ab=ab: e.activation(ab[:, cs_(n)], tr, AF.Exp,
                                                              scale=dv[:, HC + c:HC + c + 1],
                                                              bias=dv[:, HC + c:HC + c + 1]),
                     reads=["tr%d" % tn, "dv"], writes=["a%d_%d" % (cp, n)])
                P.op(ACT, lambda e, c=c, n=n, tr=tr, wb=wb: e.activation(wb[:, cs_(n)], tr, AF.Tanh,
                                                              scale=dv[:, QC + c:QC + c + 1],
                                                              bias=dv[:, QC + c:QC + c + 1]),
                     reads=["tr%d" % tn, "dv"], writes=["w%d_%d" % (cp, n)])

            def sG(q):
                c, n = divmod(q, 4)
                ti, tn = tib[q % 2], q % 2
                ab, cp = ab2[c % 2], c % 2
                P.op(DVE, lambda e, n=n, ti=ti: e.scalar_tensor_tensor(ixb[:, cs_(n)], ti, 1.0, xcb[:, cs_(n)],
                                                                   ALU.add, ALU.mult),
                     reads=["ti%d" % tn, "xc%d" % n], writes=["ix%d" % n])
                P.op(DVE, lambda e, n=n, ab=ab: e.scalar_tensor_tensor(ixb[:, cs_(n)], ab[:, cs_(n)], 1.0, ixb[:, cs_(n)],
                                                                   ALU.add, ALU.mult),
                     reads=["a%d_%d" % (cp, n)], writes=["ix%d" % n])

            def sL1(c):
                wb, cp = wb2[c % 2], c % 2
                for n in range(4):
                    P.op(ACT, lambda e, n=n, wb=wb: e.activation(wb[:, cs_(n)], wb[:, cs_(n)], AF.Sqrt, scale=-0.25),
                         reads=["w%d_%d" % (cp, n)], writes=["w%d_%d" % (cp, n)])

            def sL2(c):
                wb, cp = wb2[c % 2], c % 2
                for n in range(4):
                    P.op(DVE if c == 7 else POOL,
                         lambda e, n=n, wb=wb: e.tensor_tensor(bxb[:, cs_(n)], wb[:, cs_(n)], ixb[:, cs_(n)], ALU.mult),
                         reads=["w%d_%d" % (cp, n), "ix%d" % n], writes=["bx%d" % n])

            def sL3(c):
                ab, cp = ab2[c % 2], c % 2
                for n in range(4):
                    h = hb2[n % 2]
                    if n == 0:
                        init = hinit[:, c:c + 1] if main else 0.0
                        ird = ["hinit"] if main else []
                    else:
                        init = hb2[(n - 1) % 2][:, 511:512]
                        ird = ["hb%d" % ((n - 1) % 2)]
                    P.op(DVE, lambda e, n=n, init=init, h=h, ab=ab: e.tensor_tensor_scan(h, ab[:, cs_(n)], bxb[:, cs_(n)],
                                                                                   init, ALU.mult, ALU.add),
                         reads=["a%d_%d" % (cp, n), "bx%d" % n] + ird, writes=["hb%d" % (n % 2)])
                    if main:
                        P.op(DVE if c == 7 else POOL,
                             lambda e, n=n, c=c, h=h: e.tensor_tensor(BT[:, c, cs_(n)], h, BT[:, c, cs_(n)], ALU.mult),
                             reads=["hb%d" % (n % 2)], writes=["BT%d_%d" % (c, n)])
                if not main:
                    P.op(DVE, lambda e, c=c: e.tensor_copy(hlast[:, c:c + 1], hb2[1][:, 511:512]),
                         reads=["hb1"], writes=["hlast"])

            def sHist(c):
                P.op(DVE, lambda e, c=c: e.tensor_copy(hist[:, c, :], xrb[:, 2048:2051]),
                     reads=["xr3"], writes=["hist"])

            def ok(x):
                return 0 <= x < NQ

            for k in range(NQ + 12):
                if ok(k - 4):
                    sE(k - 4)
                if ok(k - 2):
                    sC(k - 2)
                if ok(k):
                    sA(k)
                cL3 = (k - 11) // 4
                if (k - 11) % 4 == 0 and 0 <= cL3 < 8:
                    sL3(cL3)
                cL2 = (k - 10) // 4
                if (k - 10) % 4 == 0 and 0 <= cL2 < 8:
                    sL2(cL2)
                if ok(k - 6):
                    sG(k - 6)
                if ok(k - 3):
                    sD(k - 3)
                cH = (k - 5) // 4
                if (not main) and (k - 5) % 4 == 0 and 0 <= cH < 8:
                    sHist(cH)
                if ok(k - 1):
                    sB(k - 1)
                cL1 = (k - 9) // 4
                if (k - 9) % 4 == 0 and 0 <= cL1 < 8:
                    sL1(cL1)
                if ok(k - 5):
                    sF(k - 5)
            return BT

        def gelu_phase(WOFF, preloaded=False):
            BT = view(144 * KB, [128, 8, 2048], BF16)
            wg = [view(WOFF + 4096 + i * 2048, [128, 8, 128], BF16) for i in range(2)]
            for c in range(8):
                b = c % 2
                if not (preloaded and c == 0):
                    P.dma(POOL, wg[b], wslice(w_in, 4 * D + c * 128, 128), writes=["wg%d" % b])
                for n in range(4):
                    ps, pr = next_pg()
                    for k in range(8):
                        P.op(PE, lambda e, ps=ps, k=k, n=n, b=b: e.matmul(ps, wg[b][:, k, :],
                                                                      XT[:, k, 512 + n * 512: 512 + (n + 1) * 512],
                                                                      start=(k == 0), stop=(k == 7)),
                             reads=["wg%d" % b], writes=[pr])
                    P.op(ACT, lambda e, ps=ps, c=c, n=n: e.activation(BT[:, c, n * 512:(n + 1) * 512], ps, AF.Gelu_apprx_tanh),
                         reads=[pr], writes=["BT%d_%d" % (c, n)])

        rnn_pass(False, 84 * KB, 176 * KB, pre0=True)
        P.barrier()

        AT = view(84 * KB, [128, 8, 2048], BF16)
        o = 116 * KB
        QT = [view(o + i * 4096, [128, 2048], BF16) for i in range(2)]; o += 8192
        KT = [view(o + i * 5120, [128, 2560], BF16) for i in range(2)]; o += 10240
        BS = [view(o + i * 5120, [128, 2, 5, 128], F32) for i in range(2)]; o += 10240
        SS = [view(o + i * 2560, [128, 5, 128], F32) for i in range(3)]; o += 7680
        PT = [view(o + i * 1280, [128, 5, 128], BF16) for i in range(3)]; o += 3840
        AH = [view(o + i * 4096, [128, 16, 128], BF16) for i in range(2)]; o += 8192
        WQ = [view(o + i * 2048, [128, 8, 128], BF16) for i in range(2)]; o += 4096
        WK = [view(o + i * 2048, [128, 8, 128], BF16) for i in range(2)]; o += 4096
        assert o <= 176 * KB
        Sb = [psum[:, k * 1024:k * 1024 + 640].rearrange("p (a b) -> p a b", a=5) for k in range(2)]
        Ob = [psum[:, (4 + k) * 512:(4 + k) * 512 + 65] for k in range(2)]
        pg_avail[0] = [6, 7]
        rsum = cpool[:, 648:651]

        def load_w(hp):
            b = hp % 2
            P.dma(POOL, WQ[b], wslice(w_in, hp * 128, 128), writes=["WQ%d" % b])
            P.dma(POOL, WK[b], wslice(w_in, D + hp * 128, 128), writes=["WK%d" % b])
            P.dma(SP, BS[b], biasT[hp].rearrange("p (h k q) -> p h k q", h=2, k=5), writes=["BS%d" % b])

        def proj_chunk(hp, n):
            b = hp % 2
            ps, pr = next_pg()
            if n < 4:
                W, wr, dst, dr, c0 = WQ[b], "WQ%d" % b, QT[b][:, n * 512:(n + 1) * 512], "QT%d" % b, 512 + n * 512
            else:
                m = n - 4
                W, wr, dst, dr, c0 = WK[b], "WK%d" % b, KT[b][:, m * 512:(m + 1) * 512], "KT%d" % b, m * 512
            for k in range(8):
                P.op(PE, lambda e, ps=ps, k=k, W=W, c0=c0: e.matmul(ps, W[:, k, :], XT[:, k, c0:c0 + 512],
                                                                 start=(k == 0), stop=(k == 7)),
                     reads=[wr], writes=[pr])
            P.op(ACT, lambda e, ps=ps, dst=dst: e.copy(dst, ps), reads=[pr], writes=[dr])

        def st_S(i):
            hp, j, hh = i // 32, (i % 32) // 2, i % 2
            b = hp % 2
            k2 = i % 2
            rows = slice(hh * 64, hh * 64 + 64)
            for kt in range(5):
                P.op(PE, lambda e, k2=k2, kt=kt, j=j, rows=rows, b=b: e.matmul(
                    Sb[k2][:, kt, :], KT[b][rows, (j + kt) * 128:(j + kt + 1) * 128],
                    QT[b][rows, j * 128:(j + 1) * 128], start=True, stop=True),
                    reads=["QT%d" % b, "KT%d" % b], writes=["pb%d" % k2])

        def st_B(i):
            hp, j, hh = i // 32, (i % 32) // 2, i % 2
            b = hp % 2
            kb = i % 3
            k2 = i % 2
            P.op(DVE, lambda e, kb=kb, k2=k2, b=b, hh=hh: e.scalar_tensor_tensor(SS[kb], Sb[k2], 0.125, BS[b][:, hh],
                                                                          ALU.mult, ALU.add),
                 reads=["pb%d" % k2, "BS%d" % b], writes=["SS%d" % kb])
            nh = max(0, 4 - j)
            if nh > 0:
                P.op(ACT, lambda e, kb=kb, nh=nh: e.activation(PT[kb][:, 0:nh, :], SS[kb][:, 0:nh, :], AF.Exp,
                                                             bias=pvt[:, 1:2]),
                     reads=["SS%d" % kb, "pv"], writes=["PT%d" % kb])
            P.op(ACT, lambda e, kb=kb, nh=nh: e.activation(PT[kb][:, nh:5, :], SS[kb][:, nh:5, :], AF.Exp),
                 reads=["SS%d" % kb], writes=["PT%d" % kb])

        def st_C(i):
            hp, j, hh = i // 32, (i % 32) // 2, i % 2
            b = hp % 2
            kb = i % 3
            k2 = i % 2
            for kt in range(5):
                P.op(PE, lambda e, kb=kb, k2=k2, kt=kt, j=j, hp=hp, hh=hh: e.matmul(
                    Ob[k2], PT[kb][:, kt, :], V[:, j + kt, hp * 2 + hh, :],
                    start=(kt == 0), stop=(kt == 4)),
                    reads=["PT%d" % kb, "V%d_%d" % (j + kt, hp // 4), "Vones"], writes=["po%d" % k2])
            P.op(DVE, lambda e, kb=kb, k2=k2: e.reciprocal(rsum[:, kb:kb + 1], Ob[k2][:, 64:65]),
                 reads=["po%d" % k2], writes=["rs_%d" % kb])
            P.op(ACT, lambda e, kb=kb, k2=k2, hh=hh, b=b, j=j: e.activation(
                AH[b][:, j, hh * 64:(hh + 1) * 64], Ob[k2][:, 0:64], AF.Copy, scale=rsum[:, kb:kb + 1]),
                reads=["po%d" % k2, "rs_%d" % kb], writes=["AH%d" % b])

        def st_T(hp):
            b = hp % 2
            for g in range(2):
                ps, pr = next_pg()
                pv3 = pg_bf(ps)
                for i in range(8):
                    P.op(PE, lambda e, pv3=pv3, i=i, g=g, b=b: e.transpose(pv3[:, i, :], AH[b][:, g * 8 + i, :], identb[:]),
                         reads=["AH%d" % b, "ident"], writes=[pr])
                P.op(ACT, lambda e, pv3=pv3, g=g, hp=hp: e.copy(
                    AT[:, hp, g * 1024:(g + 1) * 1024].rearrange("p (a b) -> p a b", a=8), pv3),
                    reads=[pr], writes=["AT%d_%d" % (hp, g)])

        NIT = 8 * 16 * 2
        pg_avail[0] = [6, 7]
        load_w(0)
        for n in range(9):
            proj_chunk(0, n)
        st_S(0)
        st_B(0)
        st_S(1)
        st_B(1)
        for i in range(NIT):
            hp, r = divmod(i, 32)
            if r == 0 and hp + 1 < 8:
                load_w(hp + 1)
            if r % 2 == 0 and 4 <= r < 22 and hp + 1 < 8:
                proj_chunk(hp + 1, (r - 4) // 2)
            if r == 2 and hp >= 1:
                st_T(hp - 1)
            if i + 2 < NIT:
                st_S(i + 2)
                st_B(i + 2)
            st_C(i)
        branch_pre(w_ao, 5 * D, 172 * KB)
        st_T(7)
        pg_avail[0] = list(range(8))
        P.barrier()

        GT = view(40 * KB, [128, 8, 2048], BF16)

        def branch(wsrc, gate_c0, src3, bcol, first, WOFF, TOFF, preloaded=False, hook=None):
            WA = [view(WOFF + i * 2048, [128, 8, 128], BF16) for i in range(2)]
            WG = [view(WOFF + 4096 + i * 2048, [128, 8, 128], BF16) for i in range(2)]
            sgb = [view(TOFF + i * 2048, [128, 512], F32) for i in range(2)]
            tmb = [view(TOFF + 4096 + i * 2048, [128, 512], F32) for i in range(2)]

            def loadw(oc):
                wb_ = oc % 2
                P.dma(POOL, WA[wb_], wslice(wsrc, oc * 128, 128), writes=["WA%d" % wb_])
                P.dma(POOL, WG[wb_], wslice(w_in, gate_c0 + oc * 128, 128), writes=["WG%d" % wb_])
            if not preloaded:
                loadw(0)
            it = 0
            for oc in range(8):
                wb_ = oc % 2
                if oc + 1 < 8:
                    loadw(oc + 1)
                if oc == 0 and hook is not None:
                    hook()
                for n in range(4):
                    cs = slice(n * 512, (n + 1) * 512)
                    b = it % 2
                    it += 1
                    psG, prG = next_pg()
                    for k in range(8):
                        P.op(PE, lambda e, psG=psG, k=k, wb_=wb_, n=n: e.matmul(psG, WG[wb_][:, k, :],
                                                                            XT[:, k, 512 + n * 512:512 + (n + 1) * 512],
                                                                            start=(k == 0), stop=(k == 7)),
                             reads=["WG%d" % wb_], writes=[prG])
                    psY, prY = next_pg()
                    for k in range(8):
                        P.op(PE, lambda e, psY=psY, k=k, wb_=wb_, cs=cs: e.matmul(psY, WA[wb_][:, k, :],
                                                                              src3[:, k, cs], start=(k == 0), stop=(k == 7)),
                             reads=["WA%d" % wb_, "SRC"], writes=[prY])
                    P.op(ACT, lambda e, psG=psG, b=b, oc=oc: e.activation(sgb[b], psG, AF.Sigmoid,
                                                                      bias=cv[:, bcol + oc:bcol + oc + 1]),
                         reads=[prG, "cv"], writes=["sg%d" % b])
                    if first:
                        P.op(DVE, lambda e, psY=psY, b=b, oc=oc, cs=cs: e.tensor_tensor(GT[:, oc, cs], sgb[b], psY, ALU.mult),
                             reads=[prY, "sg%d" % b], writes=["GT"])
                    else:
                        P.op(DVE, lambda e, psY=psY, b=b: e.tensor_tensor(tmb[b], sgb[b], psY, ALU.mult),
                             reads=[prY, "sg%d" % b], writes=["tm%d" % b])
                        P.op(POOL, lambda e, b=b, oc=oc, cs=cs: e.tensor_tensor(GT[:, oc, cs], tmb[b], GT[:, oc, cs], ALU.add),
                             reads=["tm%d" % b], writes=["GT"])

        branch(w_ao, 5 * D, AT, BMA, True, 172 * KB, 116 * KB, preloaded=True)
        P.dma(POOL, view(180 * KB, [128, 8, 128], BF16), wslice(w_in, 4 * D, 128), writes=["wg0"])
        P.barrier()
        gelu_phase(176 * KB, preloaded=True)
        BT = rnn_pass(True, 72 * KB, 176 * KB)
        P.barrier()
        WO = view(88 * KB, [128, 8, 1024], BF16)
        branch(w_ro, 6 * D, BT, BMB, False, 72 * KB, 80 * KB,
               hook=lambda: P.dma(POOL, WO, wslice(w_o, 0, D), writes=["WO"]))
        xt2 = [view(104 * KB + i * 4096, [128, D], F32) for i in range(2)]
        for t in range(2):
            P.dma(SP, xt2[t], xm[t * 128:(t + 1) * 128, :], writes=["x2_%d" % t])
        P.barrier()

        H = view(120 * KB, [128, 16, D], F32)
        xt2 = [view(104 * KB + i * 4096, [128, D], F32) for i in range(2)]
        HT = view(0, [128, 8, 2048], BF16)
        g2 = view(72 * KB, [128, D], F32)
        g3 = view(100 * KB, [128, D], F32)
        xn2 = [view(112 * KB + i * 2048, [128, D], BF16) for i in range(3)]
        junkW = view(118 * KB, [128, D], BF16)
        P.dma(SP, g2, gains[1:2, :].broadcast_to([128, D]), writes=["g"])
        P.op(DVE, lambda e: e.memset(ss, 0.0), writes=["ss%d" % i for i in range(32)])
        for t in range(16):
            b = t % 2
            if t >= 2:
                P.dma(SP, xt2[b], xm[t * 128:(t + 1) * 128, :], writes=["x2_%d" % b])
            for cc in range(2):
                ps, pr = next_pg()
                for k in range(8):
                    P.op(PE, lambda e, ps=ps, k=k, t=t, cc=cc: e.matmul(ps, GT[:, k, t * 128:(t + 1) * 128],
                                                                    WO[:, k, cc * 512:(cc + 1) * 512],
                                                                    start=(k == 0), stop=(k == 7)),
                         reads=["WO"], writes=[pr])
                P.op(DVE, lambda e, ps=ps, t=t, cc=cc, b=b: e.tensor_tensor(H[:, t, cc * 512:(cc + 1) * 512], ps,
                                                                        xt2[b][:, cc * 512:(cc + 1) * 512], ALU.add),
                     reads=[pr, "x2_%d" % b], writes=["H%d" % t])
            norm_s(H[:, t, :], "H%d" % t, t, junkW)
            if t >= 1:
                norm_m(H[:, t - 1, :], "H%d" % (t - 1), g2, t - 1, xn2[(t - 1) % 3], "xn%d" % ((t - 1) % 3))
            if t >= 2:
                norm_b(HT[:, :, (t - 2) * 128:(t - 1) * 128], "HT%d" % (t - 2), xn2[(t - 2) % 3], "xn%d" % ((t - 2) % 3))
        norm_m(H[:, 15, :], "H15", g2, 15, xn2[15 % 3], "xn%d" % (15 % 3))
        norm_b(HT[:, :, 14 * 128:15 * 128], "HT14", xn2[14 % 3], "xn%d" % (14 % 3))
        norm_b(HT[:, :, 15 * 128:16 * 128], "HT15", xn2[15 % 3], "xn%d" % (15 % 3))
        P.dma(POOL, view(32 * KB, [128, 8, 128], BF16), wslice(w_fi, 0, 128), writes=["WIg0"])
        P.dma(POOL, view(32 * KB + 2048, [128, 8, 128], BF16), wslice(w_fi, DFF, 128), writes=["WIu0"])
        P.barrier()

        WI = [view(32 * KB + i * 2048, [128, 8, 128], BF16) for i in range(4)]
        FT = view(40 * KB, [128, 8, 2048], BF16)
        WOUT = view(72 * KB, [128, 8, 1024], BF16)
        slb = [view(88 * KB + i * 2048, [128, 512], F32) for i in range(2)]
        ob = [view(92 * KB + i * 4096, [128, D], F32) for i in range(2)]
        P.dma(SP, g3, gains[2:3, :].broadcast_to([128, D]), writes=["g3"])
        stores = []
        junk3 = xn2[0]

        def final_s(t):
            si = 16 + t
            P.op(ACT, lambda e, t=t, si=si: e.activation(junk3, H[:, t, :], AF.Square, accum_out=ss[:, si:si + 1]),
                 reads=["H%d" % t], writes=["ss%d" % si])
            P.op(ACT, lambda e, si=si: e.activation(rs[:, si:si + 1], ss[:, si:si + 1], AF.Sqrt,
                                                   scale=1.0 / D, bias=cst[:, 1:2]),
                 reads=["ss%d" % si, "cst"], writes=["rs%d" % si])

        def final_m(t):
            si = 16 + t
            b = t % 2
            P.op(DVE, lambda e, si=si: e.reciprocal(rs[:, si:si + 1], rs[:, si:si + 1]),
                 reads=["rs%d" % si], writes=["rs%d" % si])
            P.op(DVE, lambda e, t=t, si=si, b=b: e.scalar_tensor_tensor(ob[b], H[:, t, :], rs[:, si:si + 1], g3,
                                                                   ALU.mult, ALU.mult),
                 reads=["H%d" % t, "rs%d" % si, "g3"], writes=["ob%d" % b])
            stores.append(P.dma(SP, out[t * 128:(t + 1) * 128, :], ob[b], reads=["ob%d" % b]))

        fi = 0
        for (f0, f1) in FGROUPS:
            nf = f1 - f0
            for f in range(f0, f1):
                fl = f - f0
                b = fi % 2
                fi += 1
                if f > 0:
                    P.dma(POOL, WI[2 * b], wslice(w_fi, f * 128, 128), writes=["WIg%d" % b])
                    P.dma(POOL, WI[2 * b + 1], wslice(w_fi, DFF + f * 128, 128), writes=["WIu%d" % b])
                P.dma(POOL, WOUT[:, fl, :], w_fo[f * 128:(f + 1) * 128, :], writes=["WOUT%d" % fl])
                for n in range(4):
                    cs = slice(n * 512, (n + 1) * 512)
                    psG, prG = next_pg()
                    for k in range(8):
                        P.op(PE, lambda e, psG=psG, k=k, cs=cs, b=b: e.matmul(psG[:], WI[2 * b][:, k, :], HT[:, k, cs],
                                                                         start=(k == 0), stop=(k == 7)),
                             reads=["WIg%d" % b, "HT"], writes=[prG])
                    psU, prU = next_pg()
                    for k in range(8):
                        P.op(PE, lambda e, psU=psU, k=k, cs=cs, b=b: e.matmul(psU[:], WI[2 * b + 1][:, k, :], HT[:, k, cs],
                                                                         start=(k == 0), stop=(k == 7)),
                             reads=["WIu%d" % b, "HT"], writes=[prU])
                    sb2 = n % 2
                    P.op(ACT, lambda e, psG=psG, sb2=sb2: e.activation(slb[sb2], psG[:], AF.Silu),
                         reads=[prG], writes=["sl%d" % sb2])
                    P.op(DVE, lambda e, psU=psU, sb2=sb2, fl=fl, cs=cs: e.tensor_tensor(FT[:, fl, cs], slb[sb2], psU[:], ALU.mult),
                         reads=[prU, "sl%d" % sb2], writes=["FT"])
            for t in range(16):
                for cc in range(2):
                    ps, pr = next_pg()
                    for fl in range(nf):
                        P.op(PE, lambda e, ps=ps, fl=fl, t=t, cc=cc, nf=nf: e.matmul(
                            ps[:], FT[:, fl, t * 128:(t + 1) * 128], WOUT[:, fl, cc * 512:(cc + 1) * 512],
                            start=(fl == 0), stop=(fl == nf - 1)),
                            reads=["FT", "WOUT%d" % fl], writes=[pr])
                    P.op(DVE, lambda e, ps=ps, t=t, cc=cc: e.tensor_tensor(H[:, t, cc * 512:(cc + 1) * 512], ps[:],
                                                                       H[:, t, cc * 512:(cc + 1) * 512], ALU.add),
                         reads=[pr], writes=["H%d" % t])
                if f1 == FGROUPS[-1][1]:
                    final_s(t)
                    if t >= 1:
                        final_m(t - 1)
        final_m(15)
        P.emit(final_waits=stores)
        build_program.stats = P.stats
    return nc


def _chan(v):
    return np.ascontiguousarray(np.asarray(v, np.float32).reshape(8, 128).T)


def _bias_tiles(rel_table):
    tab = np.concatenate([np.asarray(rel_table, np.float32), np.full((NH, 1), NEG, np.float32)], axis=1)
    m = np.arange(640)[:, None]
    i = np.arange(128)[None, :]
    dist = i + 512 - m
    idx = np.clip(dist, -128, 128) + 128
    kc = m // 64
    qc = i // 64
    ok = (kc >= qc) & (kc <= qc + 8)
    idx = np.where(ok, idx, 257)
    full = tab[:, idx]
    full = full.reshape(8, 2, 5, 128, 128).transpose(0, 3, 1, 2, 4)
    return np.ascontiguousarray(full).reshape(8, 128, 2 * 5 * 128)


def _blockdiag(w):
    w = np.asarray(w, np.float32)
    o = np.zeros((128, 8, 128), np.float32)
    for c in range(8):
        o[0:64, c, 0:64] = w[2 * c]
        o[64:128, c, 64:128] = w[2 * c + 1]
    return o


_NC_CACHE = {}


def kernel(x, norm_mix_g, w_in, b_merge, rel_table, w_attn_out, conv_w, conv_b,
           w_rg_r, b_rg_r, w_rg_i, b_rg_i, rg_lambda, w_rnn_out, w_o,
           norm_ffn_g, w_ffn_in, w_ffn_out, final_norm_g):
    x = np.asarray(x, np.float32)
    f = lambda a: np.ascontiguousarray(np.asarray(a, np.float32))
    cvec = np.zeros((128, NV), np.float32)
    cw = np.asarray(conv_w, np.float32)[0]
    for j in range(4):
        cvec[:, CW + 8 * j: CW + 8 * j + 8] = _chan(cw[j])
    cvec[:, CB:CB + 8] = _chan(np.asarray(conv_b)[0])
    cvec[:, BR:BR + 8] = _chan(np.asarray(b_rg_r)[0])
    cvec[:, BI:BI + 8] = _chan(np.asarray(b_rg_i)[0])
    cvec[:, LAM:LAM + 8] = _chan(np.asarray(rg_lambda)[0])
    bm = np.asarray(b_merge, np.float32)[0]
    cvec[:, BMA:BMA + 8] = _chan(bm[:D])
    cvec[:, BMB:BMB + 8] = _chan(bm[D:])
    gains = np.stack([np.asarray(norm_mix_g, np.float32)[0], np.asarray(norm_ffn_g, np.float32)[0],
                      np.asarray(final_norm_g, np.float32)], 0)
    shared = dict(
        w_in=f(np.asarray(w_in)[0]), w_ao=f(np.asarray(w_attn_out)[0]), w_ro=f(np.asarray(w_rnn_out)[0]),
        w_o=f(np.asarray(w_o)[0]), w_fi=f(np.asarray(w_ffn_in)[0]), w_fo=f(np.asarray(w_ffn_out)[0]),
        wr_bd=_blockdiag(np.asarray(w_rg_r)[0]), wi_bd=_blockdiag(np.asarray(w_rg_i)[0]),
        cvec=cvec, gains=np.ascontiguousarray(gains), biasT=_bias_tiles(np.asarray(rel_table)[0]),
        cbrow=f(np.asarray(conv_b)[0].reshape(1, D)),
    )
    in_maps = []
    for c in range(8):
        b, hf = c // 2, c % 2
        m = dict(shared)
        m["xm"] = np.ascontiguousarray(x[b, hf * TM:(hf + 1) * TM])
        pvv = np.zeros((128, 2), np.float32)
        if hf == 1:
            m["xp"] = np.ascontiguousarray(x[b, 0:TPRE])
            pvv[:, 0] = 1.0
        else:
            m["xp"] = np.zeros((TPRE, D), np.float32)
            pvv[:, 1] = NEG
        m["pv"] = pvv
        in_maps.append(m)
    if "nc" not in _NC_CACHE:
        _NC_CACHE["nc"] = build_program()
    nc = _NC_CACHE["nc"]
    res = run_bass_kernel_spmd(nc, in_maps, core_ids=list(range(8)))
    outp = np.zeros((4, 4096, D), np.float32)
    for c in range(8):
        b, hf = c // 2, c % 2
        outp[b, hf * TM:(hf + 1) * TM] = res.results[c]["out"]
    return outp


if __name__ == "__main__":
    import time
    t0 = time.time()
    nc = build_program()
    print("build ok", time.time() - t0, build_program.stats)
```

```python
import contextlib
import numpy as np
import concourse.bass as bass
import concourse.mybir as mybir
from concourse.bass_utils import run_bass_kernel_spmd

F32 = mybir.dt.float32
BF16 = mybir.dt.bfloat16
AF = mybir.ActivationFunctionType
ALU = mybir.AluOpType

PE, ACT, DVE, POOL, SP = "tensor", "scalar", "vector", "gpsimd", "sync"
ENGINES = [PE, ACT, DVE, POOL, SP]
NDMA_SEM = 12

D = 1024
NH = 16
DFF = 2816
TM = 2048
TPRE = 2048
HALO = 512
NEG = -1e30
FGROUPS = [(0, 8), (8, 15), (15, 22)]

CW, CB, BR, BI, LAM, BMA, BMB, NV = 0, 32, 40, 48, 56, 64, 72, 80


class Res:
    __slots__ = ("name", "w", "r")

    def __init__(self, name=""):
        self.name = name
        self.w = None
        self.r = []


class Op:
    __slots__ = ("eng", "fn", "reads", "writes", "kind", "deps", "signal",
                 "tick", "sem", "gidx")

    def __init__(self, eng, fn, reads, writes, kind):
        self.eng = eng
        self.fn = fn
        self.reads = reads
        self.writes = writes
        self.kind = kind
        self.deps = []
        self.signal = False
        self.tick = None
        self.sem = None


class Prog:
    def __init__(self, nc):
        self.nc = nc
        self.ops = []
        self.res = {}

    def R(self, name):
        r = self.res.get(name)
        if r is None:
            r = self.res[name] = Res(name)
        return r

    def _rs(self, lst):
        return [self.R(x) if isinstance(x, str) else x for x in lst]

    def op(self, eng, fn, reads=(), writes=()):
        o = Op(eng, fn, self._rs(reads), self._rs(writes), "op")
        self._add(o)
        return o

    def dma(self, eng, out, in_, reads=(), writes=()):
        o = Op(eng, (out, in_), self._rs(reads), self._rs(writes), "dma")
        self._add(o)
        return o

    def barrier(self):
        lb = getattr(self, "_last_barrier", -1)
        last = {}
        dmas = {SP: [], POOL: []}
        for o in self.ops:
            if o.kind == "dma":
                dmas[o.eng].append(o)
            elif o.kind == "op" and o.gidx > lb:
                last[o.eng] = o
        deps = list(last.values()) + dmas[SP][-NDMA_SEM:] + dmas[POOL][-NDMA_SEM:]
        for e in ENGINES:
            o = Op(e, (lambda eng: eng.nop(nofuse=True)), [], [], "nop")
            o.gidx = len(self.ops)
            o.deps = [d for d in deps if d.kind == "dma" or d.eng != e]
            for d in o.deps:
                d.signal = True
            self.ops.append(o)
        self._last_barrier = len(self.ops) - 1
        for r in self.res.values():
            r.w = None
            r.r = []

    def _add(self, o):
        o.gidx = len(self.ops)
        deps = {}
        for r in o.reads:
            if r.w is not None:
                deps[r.w.gidx] = r.w
        for w in o.writes:
            if w.w is not None:
                deps[w.w.gidx] = w.w
            lastr = {}
            for rd in w.r:
                if rd.kind == "dma":
                    deps[rd.gidx] = rd
                else:
                    lastr[rd.eng] = rd
            for rd in lastr.values():
                deps[rd.gidx] = rd
        dl = []
        for d in deps.values():
            if d is o:
                continue
            if d.kind != "dma" and o.kind != "dma" and d.eng == o.eng and o.eng == PE:
                continue
            dl.append(d)
        o.deps = dl
        for d in dl:
            d.signal = True
        for r in o.reads:
            r.r.append(o)
        for w in o.writes:
            w.w = o
            w.r = []
        self.ops.append(o)

    def emit(self, final_waits=()):
        nc = self.nc
        for o in final_waits:
            o.signal = True
        with contextlib.ExitStack() as es:
            esem = {e: es.enter_context(nc.semaphore("s_" + e)) for e in ENGINES}
            dsem = {e: [es.enter_context(nc.semaphore("d_%s_%d" % (e, i)))
                        for i in range(NDMA_SEM)] for e in (SP, POOL)}
            cnt = {e: 0 for e in ENGINES}
            dcnt = {e: 0 for e in (SP, POOL)}
            per_eng = {e: [] for e in ENGINES}
            for o in self.ops:
                if o.kind == "dma":
                    n = dcnt[o.eng]
                    dcnt[o.eng] += 1
                    o.sem = dsem[o.eng][n % NDMA_SEM]
                    o.tick = 16 * (n // NDMA_SEM + 1)
                elif o.signal:
                    cnt[o.eng] += 1
                    o.tick = cnt[o.eng]
                    o.sem = esem[o.eng]
                per_eng[o.eng].append(o)
            self.stats = {e: len(per_eng[e]) for e in ENGINES}
            self.stats["ticks"] = dict(cnt)
            self.stats["dmas"] = dict(dcnt)
            block = es.enter_context(nc.Block())

            def make(e):
                ops = per_eng[e]

                def body(eng):
                    seen = {}
                    ring_last = {}

                    def wait(sem, val):
                        k = id(sem)
                        if seen.get(k, 0) >= val:
                            return
                        eng.wait_ge(sem, val)
                        seen[k] = val

                    for o in ops:
                        for d in o.deps:
                            wait(d.sem, d.tick)
                        if o.kind == "dma":
                            k = id(o.sem)
                            if k in ring_last:
                                wait(o.sem, ring_last[k])
                            ring_last[k] = o.tick
                            out, in_ = o.fn
                            eng.dma_start(out=out, in_=in_).then_inc(o.sem, 16)
                        else:
                            ins = o.fn(eng)
                            if o.signal:
                                ins.then_inc(o.sem, 1)
                    if e == SP:
                        for o in final_waits:
                            wait(o.sem, o.tick)
                return body

            for e in ENGINES:
                getattr(block, e)(make(e))


def build_program(debug=False):
    nc = bass.Bass("TRN2", target_bir_lowering=False)

    def din(name, shape):
        return nc.dram_tensor(name, shape, F32, kind="ExternalInput").ap()

    xm = din("xm", [TM, D])
    xp = din("xp", [TPRE, D])
    w_in = din("w_in", [D, 7 * D])
    w_ao = din("w_ao", [D, D])
    w_ro = din("w_ro", [D, D])
    w_o = din("w_o", [D, D])
    w_fi = din("w_fi", [D, 2 * DFF])
    w_fo = din("w_fo", [DFF, D])
    wr_bd = din("wr_bd", [128, 8, 128])
    wi_bd = din("wi_bd", [128, 8, 128])
    cvec = din("cvec", [128, NV])
    gains = din("gains", [3, D])
    biasT = din("biasT", [8, 128, 2 * 5 * 128])
    pv = din("pv", [128, 2])
    cbrow_d = din("cbrow", [1, D])
    out = nc.dram_tensor("out", [TM, D], F32, kind="ExternalOutput").ap()

    KB = 1024
    with contextlib.ExitStack() as es:
        arena = es.enter_context(nc.sbuf_tensor("arena", [128, 94 * KB], BF16))
        cpool = es.enter_context(nc.sbuf_tensor("cpool", [128, 1024], F32))
        identb = es.enter_context(nc.sbuf_tensor("identb", [128, 128], BF16))
        wbd = es.enter_context(nc.sbuf_tensor("wbd", [128, 2, 8, 128], BF16))
        dg = es.enter_context(nc.sbuf_tensor("dg", [128, 8, 4, 128], BF16))
        cbrow = es.enter_context(nc.sbuf_tensor("cbrow_sb", [1, D], BF16))
        onesr = es.enter_context(nc.sbuf_tensor("onesr", [1, 512], BF16))
        psum = es.enter_context(nc.psum_tensor("psum", [128, 4096], F32))
        pg = [psum[:, i * 512:(i + 1) * 512] for i in range(8)]
        pg_avail = [list(range(8))]

        def view(off, shape, dt):
            n = 1
            for s in shape[1:]:
                n *= s
            esz = 2 if dt == BF16 else 4
            a = arena[:, off // 2: off // 2 + n * esz // 2]
            if dt != BF16:
                a = a.bitcast(dt)
            if len(shape) == 3:
                a = a.rearrange("p (a b) -> p a b", a=shape[1])
            elif len(shape) == 4:
                a = a.rearrange("p (a b c) -> p a b c", a=shape[1], b=shape[2])
            return a

        cv = cpool[:, 0:NV]
        dv = cpool[:, 88:88 + 40]
        identf = cpool[:, 768:896]
        pvt = cpool[:, 128:130]
        cst = cpool[:, 132:136]
        ss = cpool[:, 160:160 + 32]
        rs = cpool[:, 192:192 + 32]
        hist = cpool[:, 224:224 + 24].rearrange("p (c j) -> p c j", c=8)
        hlast = cpool[:, 256:264]
        hinit = cpool[:, 264:272]
        iot = cpool[:, 512:640]
        osum = cpool[:, 640:644]

        P = Prog(nc)
        pgi = [0]

        def next_pg():
            av = pg_avail[0]
            i = av[pgi[0] % len(av)]
            pgi[0] += 1
            return pg[i], "pg%d" % i

        def pg_bf(ps):
            return ps.bitcast(BF16).rearrange("p (a b) -> p a b", a=8)

        def wslice(src, c0, ncols):
            return src[:, c0:c0 + ncols].rearrange("(k p) n -> p k n", p=128)

        def branch_pre(wsrc, gate_c0, WOFF):
            P.dma(POOL, view(WOFF, [128, 8, 128], BF16), wslice(wsrc, 0, 128), writes=["WA0"])
            P.dma(POOL, view(WOFF + 4096, [128, 8, 128], BF16), wslice(w_in, gate_c0, 128), writes=["WG0"])

        P.dma(SP, cv, cvec, writes=["cv"])
        P.dma(SP, pvt, pv, writes=["pv"])
        P.dma(POOL, wbd[:, 0], wr_bd, writes=["wbd"])
        P.dma(POOL, wbd[:, 1], wi_bd, writes=["wbd"])
        P.op(POOL, lambda e: e.iota(iot, [[1, 128]], base=0, channel_multiplier=-1,
                                    allow_small_or_imprecise_dtypes=True), writes=["iot"])
        P.op(DVE, lambda e: e.tensor_single_scalar(identb[:], iot, 0.0, ALU.is_equal),
             reads=["iot"], writes=["ident"])
        P.op(DVE, lambda e: e.memset(cst[:, 0:1], 1.0), writes=["cst"])
        P.op(DVE, lambda e: e.memset(cst[:, 1:2], 1e-6), writes=["cst"])
        P.op(DVE, lambda e: e.memset(cst[:, 2:3], 0.0), writes=["cst"])
        P.op(DVE, lambda e: e.memset(ss, 0.0), writes=["ss%d" % i for i in range(32)])
        P.op(DVE, lambda e: e.tensor_scalar_mul(dv[:, 0:8], cv[:, BR:BR + 8], 0.5), reads=["cv"], writes=["dv"])
        P.op(DVE, lambda e: e.tensor_scalar_mul(dv[:, 8:16], cv[:, BI:BI + 8], 0.5), reads=["cv"], writes=["dv"])
        P.op(ACT, lambda e: e.activation(dv[:, 32:40], cv[:, LAM:LAM + 8], AF.Exp, scale=-1.0),
             reads=["cv"], writes=["dvt"])
        P.op(ACT, lambda e: e.activation(dv[:, 32:40], dv[:, 32:40], AF.Ln, bias=cst[:, 0:1]),
             reads=["dvt", "cst"], writes=["dvt"])
        P.op(DVE, lambda e: e.tensor_scalar_mul(dv[:, 16:24], dv[:, 32:40], -4.0), reads=["dvt"], writes=["dv"])
        P.op(DVE, lambda e: e.tensor_scalar_mul(dv[:, 24:32], dv[:, 32:40], -2.0), reads=["dvt"], writes=["dv"])
        P.op(DVE, lambda e: e.tensor_single_scalar(identf, iot, 0.0, ALU.is_equal), reads=["iot"], writes=["identf"])
        def dg_build(c, j):
            P.op(DVE, lambda e, c=c, j=j: e.tensor_scalar_mul(dg[:, c, j, :], identf,
                                                             cv[:, CW + 8 * j + c:CW + 8 * j + c + 1]),
                 reads=["identf", "cv"], writes=["dg"])
        P.dma(POOL, cbrow[:], cbrow_d, writes=["cbrow"])
        P.op(DVE, lambda e: e.memset(onesr[:], 1.0), writes=["onesr"])

        HBR, HBI, HC, QC = 0, 8, 16, 24

        def norm_a(src_ap, src_res, gt, si, xnb, xn_res):
            junk = xnb
            P.op(ACT, lambda e: e.activation(junk, src_ap, AF.Square, accum_out=ss[:, si:si + 1]),
                 reads=[src_res], writes=[xn_res, "ss%d" % si])
            P.op(ACT, lambda e: e.activation(rs[:, si:si + 1], ss[:, si:si + 1], AF.Sqrt,
                                             scale=1.0 / D, bias=cst[:, 1:2]),
                 reads=["ss%d" % si, "cst"], writes=["rs%d" % si])
            P.op(DVE, lambda e: e.reciprocal(rs[:, si:si + 1], rs[:, si:si + 1]),
                 reads=["rs%d" % si], writes=["rs%d" % si])
            P.op(DVE, lambda e: e.scalar_tensor_tensor(xnb, src_ap, rs[:, si:si + 1], gt, ALU.mult, ALU.mult),
                 reads=[src_res, "rs%d" % si, "g"], writes=[xn_res])

        def norm_s(src_ap, src_res, si, junk):
            P.op(ACT, lambda e: e.activation(junk, src_ap, AF.Square, accum_out=ss[:, si:si + 1]),
                 reads=[src_res], writes=["ss%d" % si])
            P.op(ACT, lambda e: e.activation(rs[:, si:si + 1], ss[:, si:si + 1], AF.Sqrt,
                                             scale=1.0 / D, bias=cst[:, 1:2]),
                 reads=["ss%d" % si, "cst"], writes=["rs%d" % si])

        def norm_m(src_ap, src_res, gt, si, xnb, xn_res):
            P.op(DVE, lambda e: e.reciprocal(rs[:, si:si + 1], rs[:, si:si + 1]),
                 reads=["rs%d" % si], writes=["rs%d" % si])
            P.op(DVE, lambda e: e.scalar_tensor_tensor(xnb, src_ap, rs[:, si:si + 1], gt, ALU.mult, ALU.mult),
                 reads=[src_res, "rs%d" % si, "g"], writes=[xn_res])

        def norm_b(dst3, dst_res, xnb, xn_res):
            ps, pr = next_pg()
            pv3 = pg_bf(ps)
            for k in range(8):
                P.op(PE, lambda e, k=k: e.transpose(pv3[:, k, :], xnb[:, k * 128:(k + 1) * 128], identb[:]),
                     reads=[xn_res, "ident"], writes=[pr])
            P.op(ACT, lambda e: e.copy(dst3, pv3), reads=[pr], writes=[dst_res])

        XT = view(0, [128, 8, 2560], BF16)
        XTp = view(152 * KB, [128, 8, 1536], BF16)
        xt_p = [view(84 * KB + i * 8192, [128, 2, D], F32) for i in range(3)]
        xt_b = [xt_p[i // 2][:, i % 2, :] for i in range(6)]
        xn_b = [view(108 * KB + i * 2048, [128, D], BF16) for i in range(3)]
        gt = view(114 * KB, [128, D], F32)
        P.dma(SP, gt, gains[0:1, :].broadcast_to([128, D]), writes=["g"])

        ORDER = list(range(12, 32)) + list(range(0, 12))

        def p0_load(p):
            if p % 2:
                return
            t = ORDER[p]
            src = xp[t * 128:(t + 2) * 128, :] if t < 16 else xm[(t - 16) * 128:(t - 14) * 128, :]
            P.dma(SP, xt_p[(p % 6) // 2], src.rearrange("(a p) d -> p a d", p=128),
                  writes=["xt%d" % (p % 6), "xt%d" % (p % 6 + 1)])

        V = view(40 * KB, [128, 20, 16, 65], BF16)
        WV2 = [view(118 * KB + i * 8192, [128, 8, 512], BF16) for i in range(2)]
        for cc in range(2):
            P.dma(POOL, WV2[cc], wslice(w_in, 2 * D + cc * 512, 512), writes=["WV%d" % cc])
        P.op(DVE, lambda e: e.memset(V[:, :, :, 64:65], 1.0), writes=["Vones"])

        def v_group(t, cc):
            ps, pr = next_pg()
            for k in range(8):
                P.op(PE, lambda e, ps=ps, k=k, t=t, cc=cc: e.matmul(ps, XT[:, k, t * 128:(t + 1) * 128], WV2[cc][:, k, :],
                                                                start=(k == 0), stop=(k == 7)),
                     reads=["WV%d" % cc, "XT%d" % t], writes=[pr])
            P.op(DVE, lambda e, ps=ps, t=t, cc=cc: e.tensor_copy(V[:, t, cc * 8:(cc + 1) * 8, 0:64],
                                                             ps.rearrange("p (h d) -> p h d", h=8)),
                 reads=[pr], writes=["V%d_%d" % (t, cc)])

        def p0_dst(t):
            if t < 12:
                return XTp[:, :, t * 128:(t + 1) * 128], "XTp%d" % t
            return XT[:, :, (t - 12) * 128:(t - 11) * 128], "XT%d" % (t - 12)

        junk0 = view(134 * KB, [128, D], BF16)
        vsched = {}
        for g in range(40):
            vsched.setdefault(1 + (g * 31) // 40, []).append(g)
        for p in range(4):
            p0_load(p)
        norm_s(xt_b[0], "xt0", ORDER[0], junk0)
        norm_s(xt_b[1], "xt1", ORDER[1], junk0)
        norm_m(xt_b[0], "xt0", gt, ORDER[0], xn_b[0], "xn0")
        for p in range(32):
            if p + 4 < 32:
                p0_load(p + 4)
            if p + 2 < 32:
                norm_s(xt_b[(p + 2) % 6], "xt%d" % ((p + 2) % 6), ORDER[p + 2], junk0)
            if p + 1 < 32:
                norm_m(xt_b[(p + 1) % 6], "xt%d" % ((p + 1) % 6), gt, ORDER[p + 1], xn_b[(p + 1) % 3],
                       "xn%d" % ((p + 1) % 3))
            dst, dres = p0_dst(ORDER[p])
            norm_b(dst, dres, xn_b[p % 3], "xn%d" % (p % 3))
            for g in vsched.get(p, []):
                v_group(g // 2, g % 2)
            dg_build(p // 4, p % 4)
        for g in vsched.get(32, []):
            v_group(g // 2, g % 2)
        P.dma(POOL, view(176 * KB, [128, 8, 128], BF16), wslice(w_in, 3 * D, 128), writes=["wx0"])
        P.barrier()

        def rnn_pass(main, T0, WOFF, pre0=False):
            o = T0
            xrb = view(o, [128, 2112], BF16); o += 4224
            xcb = view(o, [128, 2048], BF16); o += 4096
            ab2 = [view(o + i * 8192, [128, 2048], F32) for i in range(2)]; o += 16384
            wb2 = [view(o + i * 8192, [128, 2048], F32) for i in range(2)]; o += 16384
            ixb = view(o, [128, 2048], F32); o += 8192
            bxb = view(o, [128, 2048], F32); o += 8192
            hb2 = [view(o + i * 2048, [128, 512], F32) for i in range(2)]; o += 4096
            trb = [view(o + i * 2048, [128, 512], F32) for i in range(2)]; o += 4096
            tib = [view(o + i * 1024, [128, 512], BF16) for i in range(2)]; o += 2048
            assert o - T0 <= 72 * KB
            wx = [view(WOFF + i * 2048, [128, 8, 128], BF16) for i in range(2)]
            BT = view(144 * KB, [128, 8, 2048], BF16)
            if not main:
                P.op(DVE, lambda e: e.memset(xrb[:, 0:3], 0.0), writes=["xrh"])
            else:
                P.op(DVE, lambda e: e.tensor_scalar_mul(hinit, hlast, pvt[:, 0:1]),
                     reads=["hlast", "pv"], writes=["hinit"])
            NQ = 32
            st = {}

            def cs_(n):
                return slice(n * 512, (n + 1) * 512)

            def sA(q):
                c, n = divmod(q, 4)
                b = c % 2
                if n == 0 and not (pre0 and c == 0):
                    P.dma(POOL, wx[b], wslice(w_in, 3 * D + c * 128, 128), writes=["wx%d" % b])
                ps, pr = next_pg()
                st[("xr", q)] = (ps, pr)
                for k in range(8):
                    if main:
                        rhs = XT[:, k, 512 + n * 512: 512 + (n + 1) * 512]
                    elif n < 3:
                        rhs = XTp[:, k, cs_(n)]
                    else:
                        rhs = XT[:, k, 0:512]
                    P.op(PE, lambda e, ps=ps, k=k, rhs=rhs, b=b: e.matmul(ps, wx[b][:, k, :], rhs,
                                                                     start=(k == 0), stop=(k == 7)),
                         reads=["wx%d" % b], writes=[pr])

            def sB(q):
                c, n = divmod(q, 4)
                ps, pr = st.pop(("xr", q))
                if main and n == 0:
                    P.op(DVE, lambda e, c=c: e.tensor_copy(xrb[:, 0:3], hist[:, c, :]),
                         reads=["hist"], writes=["xrh"])
                P.op(DVE, lambda e, ps=ps, n=n: e.tensor_copy(xrb[:, 3 + n * 512: 3 + (n + 1) * 512], ps),
                     reads=[pr], writes=["xr%d" % n])

            def sC(q):
                c, n = divmod(q, 4)
                ps, pr = next_pg()
                st[("xc", q)] = (ps, pr)
                rd = ["xr%d" % n, "dg"] + (["xr%d" % (n - 1)] if n else ["xrh"])
                for j in range(4):
                    P.op(PE, lambda e, ps=ps, j=j, c=c, n=n: e.matmul(ps, dg[:, c, j, :],
                                                                  xrb[:, n * 512 + j: n * 512 + j + 512],
                                                                  start=(j == 0), stop=False),
                         reads=rd, writes=[pr])
                P.op(PE, lambda e, ps=ps, c=c: e.matmul(ps, cbrow[0:1, c * 128:(c + 1) * 128], onesr[0:1, :],
                                                      start=False, stop=True),
                     reads=["cbrow", "onesr"], writes=[pr])

            def sD(q):
                c, n = divmod(q, 4)
                ps, pr = st.pop(("xc", q))
                P.op(DVE, lambda e, ps=ps, n=n: e.tensor_copy(xcb[:, cs_(n)], ps), reads=[pr], writes=["xc%d" % n])

            def sE(q):
                c, n = divmod(q, 4)
                psr, prr = next_pg()
                P.op(PE, lambda e, psr=psr, c=c, n=n: e.matmul(psr, wbd[:, 0, c, :], xcb[:, cs_(n)], start=True, stop=True),
                     reads=["wbd", "xc%d" % n], writes=[prr])
                psi, pri = next_pg()
                P.op(PE, lambda e, psi=psi, c=c, n=n: e.matmul(psi, wbd[:, 1, c, :], xcb[:, cs_(n)], start=True, stop=True),
                     reads=["wbd", "xc%d" % n], writes=[pri])
                st[("g", q)] = (psr, prr, psi, pri)

            def sF(q):
                c, n = divmod(q, 4)
                psr, prr, psi, pri = st.pop(("g", q))
                tr, ti, tn = trb[q % 2], tib[q % 2], q % 2
                ab, wb, cp = ab2[c % 2], wb2[c % 2], c % 2
                P.op(ACT, lambda e, psr=psr, c=c, tr=tr: e.activation(tr, psr, AF.Tanh, scale=0.5,
                                                                  bias=dv[:, HBR + c:HBR + c + 1]),
                     reads=[prr, "dv"], writes=["tr%d" % tn])
                P.op(ACT, lambda e, psi=psi, c=c, ti=ti: e.activation(ti, psi, AF.Tanh, scale=0.5,
                                                                  bias=dv[:, HBI + c:HBI + c + 1]),
                     reads=[pri, "dv"], writes=["ti%d" % tn])
                P.op(ACT, lambda e, c=c, n=n, tr=tr, ab=ab: e.activation(ab[:, cs_(n)], tr, AF.Exp,
                                                              scale=dv[:, HC + c:HC + c + 1],
                                                              bias=dv[:, HC + c:HC + c + 1]),
                     reads=["tr%d" % tn, "dv"], writes=["a%d_%d" % (cp, n)])
                P.op(ACT, lambda e, c=c, n=n, tr=tr, wb=wb: e.activation(wb[:, cs_(n)], tr, AF.Tanh,
                                                              scale=dv[:, QC + c:QC + c + 1],
                                                              bias=dv[:, QC + c:QC + c + 1]),
                     reads=["tr%d" % tn, "dv"], writes=["w%d_%d" % (cp, n)])

            def sG(q):
                c, n = divmod(q, 4)
                ti, tn = tib[q % 2], q % 2
                ab, cp = ab2[c % 2], c % 2
                P.op(DVE, lambda e, n=n, ti=ti: e.scalar_tensor_tensor(ixb[:, cs_(n)], ti, 1.0, xcb[:, cs_(n)],
                                                                   ALU.add, ALU.mult),
                     reads=["ti%d" % tn, "xc%d" % n], writes=["ix%d" % n])
                P.op(DVE, lambda e, n=n, ab=ab: e.scalar_tensor_tensor(ixb[:, cs_(n)], ab[:, cs_(n)], 1.0, ixb[:, cs_(n)],
                                                                   ALU.add, ALU.mult),
                     reads=["a%d_%d" % (cp, n)], writes=["ix%d" % n])

            def sL1(c):
                wb, cp = wb2[c % 2], c % 2
                for n in range(4):
                    P.op(ACT, lambda e, n=n, wb=wb: e.activation(wb[:, cs_(n)], wb[:, cs_(n)], AF.Sqrt, scale=-0.25),
                         reads=["w%d_%d" % (cp, n)], writes=["w%d_%d" % (cp, n)])

            def sL2(c):
                wb, cp = wb2[c % 2], c % 2
                for n in range(4):
                    P.op(DVE if c == 7 else POOL,
                         lambda e, n=n, wb=wb: e.tensor_tensor(bxb[:, cs_(n)], wb[:, cs_(n)], ixb[:, cs_(n)], ALU.mult),
                         reads=["w%d_%d" % (cp, n), "ix%d" % n], writes=["bx%d" % n])

            def sL3(c):
                ab, cp = ab2[c % 2], c % 2
                for n in range(4):
                    h = hb2[n % 2]
                    if n == 0:
                        init = hinit[:, c:c + 1] if main else 0.0
                        ird = ["hinit"] if main else []
                    else:
                        init = hb2[(n - 1) % 2][:, 511:512]
                        ird = ["hb%d" % ((n - 1) % 2)]
                    P.op(DVE, lambda e, n=n, init=init, h=h, ab=ab: e.tensor_tensor_scan(h, ab[:, cs_(n)], bxb[:, cs_(n)],
                                                                                   init, ALU.mult, ALU.add),
                         reads=["a%d_%d" % (cp, n), "bx%d" % n] + ird, writes=["hb%d" % (n % 2)])
                    if main:
                        P.op(DVE if c == 7 else POOL,
                             lambda e, n=n, c=c, h=h: e.tensor_tensor(BT[:, c, cs_(n)], h, BT[:, c, cs_(n)], ALU.mult),
                             reads=["hb%d" % (n % 2)], writes=["BT%d_%d" % (c, n)])
                if not main:
                    P.op(DVE, lambda e, c=c: e.tensor_copy(hlast[:, c:c + 1], hb2[1][:, 511:512]),
                         reads=["hb1"], writes=["hlast"])

            def sHist(c):
                P.op(DVE, lambda e, c=c: e.tensor_copy(hist[:, c, :], xrb[:, 2048:2051]),
                     reads=["xr3"], writes=["hist"])

            def ok(x):
                return 0 <= x < NQ

            for k in range(NQ + 12):
                if ok(k - 4):
                    sE(k - 4)
                if ok(k - 2):
                    sC(k - 2)
                if ok(k):
                    sA(k)
                cL3 = (k - 11) // 4
                if (k - 11) % 4 == 0 and 0 <= cL3 < 8:
                    sL3(cL3)
                cL2 = (k - 10) // 4
                if (k - 10) % 4 == 0 and 0 <= cL2 < 8:
                    sL2(cL2)
                if ok(k - 6):
                    sG(k - 6)
                if ok(k - 3):
                    sD(k - 3)
                cH = (k - 5) // 4
                if (not main) and (k - 5) % 4 == 0 and 0 <= cH < 8:
                    sHist(cH)
                if ok(k - 1):
                    sB(k - 1)
                cL1 = (k - 9) // 4
                if (k - 9) % 4 == 0 and 0 <= cL1 < 8:
                    sL1(cL1)
                if ok(k - 5):
                    sF(k - 5)
            return BT

        def gelu_phase(WOFF, preloaded=False):
            BT = view(144 * KB, [128, 8, 2048], BF16)
            wg = [view(WOFF + 4096 + i * 2048, [128, 8, 128], BF16) for i in range(2)]
            for c in range(8):
                b = c % 2
                if not (preloaded and c == 0):
                    P.dma(POOL, wg[b], wslice(w_in, 4 * D + c * 128, 128), writes=["wg%d" % b])
                for n in range(4):
                    ps, pr = next_pg()
                    for k in range(8):
                        P.op(PE, lambda e, ps=ps, k=k, n=n, b=b: e.matmul(ps, wg[b][:, k, :],
                                                                      XT[:, k, 512 + n * 512: 512 + (n + 1) * 512],
                                                                      start=(k == 0), stop=(k == 7)),
                             reads=["wg%d" % b], writes=[pr])
                    P.op(ACT, lambda e, ps=ps, c=c, n=n: e.activation(BT[:, c, n * 512:(n + 1) * 512], ps, AF.Gelu_apprx_tanh),
                         reads=[pr], writes=["BT%d_%d" % (c, n)])

        rnn_pass(False, 84 * KB, 176 * KB, pre0=True)
        P.barrier()

        AT = view(84 * KB, [128, 8, 2048], BF16)
        o = 116 * KB
        QT = [view(o + i * 4096, [128, 2048], BF16) for i in range(2)]; o += 8192
        KT = [view(o + i * 5120, [128, 2560], BF16) for i in range(2)]; o += 10240
        BS = [view(o + i * 5120, [128, 2, 5, 128], F32) for i in range(2)]; o += 10240
        SS = [view(o + i * 2560, [128, 5, 128], F32) for i in range(3)]; o += 7680
        PT = [view(o + i * 1280, [128, 5, 128], BF16) for i in range(3)]; o += 3840
        AH = [view(o + i * 4096, [128, 16, 128], BF16) for i in range(2)]; o += 8192
        WQ = [view(o + i * 2048, [128, 8, 128], BF16) for i in range(2)]; o += 4096
        WK = [view(o + i * 2048, [128, 8, 128], BF16) for i in range(2)]; o += 4096
        assert o <= 176 * KB
        Sb = [psum[:, k * 1024:k * 1024 + 640].rearrange("p (a b) -> p a b", a=5) for k in range(2)]
        Ob = [psum[:, (4 + k) * 512:(4 + k) * 512 + 65] for k in range(2)]
        pg_avail[0] = [6, 7]
        rsum = cpool[:, 648:651]

        def load_w(hp):
            b = hp % 2
            P.dma(POOL, WQ[b], wslice(w_in, hp * 128, 128), writes=["WQ%d" % b])
            P.dma(POOL, WK[b], wslice(w_in, D + hp * 128, 128), writes=["WK%d" % b])
            P.dma(SP, BS[b], biasT[hp].rearrange("p (h k q) -> p h k q", h=2, k=5), writes=["BS%d" % b])

        def proj_chunk(hp, n):
            b = hp % 2
            ps, pr = next_pg()
            if n < 4:
                W, wr, dst, dr, c0 = WQ[b], "WQ%d" % b, QT[b][:, n * 512:(n + 1) * 512], "QT%d" % b, 512 + n * 512
            else:
                m = n - 4
                W, wr, dst, dr, c0 = WK[b], "WK%d" % b, KT[b][:, m * 512:(m + 1) * 512], "KT%d" % b, m * 512
            for k in range(8):
                P.op(PE, lambda e, ps=ps, k=k, W=W, c0=c0: e.matmul(ps, W[:, k, :], XT[:, k, c0:c0 + 512],
                                                                 start=(k == 0), stop=(k == 7)),
                     reads=[wr], writes=[pr])
            P.op(ACT, lambda e, ps=ps, dst=dst: e.copy(dst, ps), reads=[pr], writes=[dr])

        def st_S(i):
            hp, j, hh = i // 32, (i % 32) // 2, i % 2
            b = hp % 2
            k2 = i % 2
            rows = slice(hh * 64, hh * 64 + 64)
            for kt in range(5):
                P.op(PE, lambda e, k2=k2, kt=kt, j=j, rows=rows, b=b: e.matmul(
                    Sb[k2][:, kt, :], KT[b][rows, (j + kt) * 128:(j + kt + 1) * 128],
                    QT[b][rows, j * 128:(j + 1) * 128], start=True, stop=True),
                    reads=["QT%d" % b, "KT%d" % b], writes=["pb%d" % k2])

        def st_B(i):
            hp, j, hh = i // 32, (i % 32) // 2, i % 2
            b = hp % 2
            kb = i % 3
            k2 = i % 2
            P.op(DVE, lambda e, kb=kb, k2=k2, b=b, hh=hh: e.scalar_tensor_tensor(SS[kb], Sb[k2], 0.125, BS[b][:, hh],
                                                                          ALU.mult, ALU.add),
                 reads=["pb%d" % k2, "BS%d" % b], writes=["SS%d" % kb])
            nh = max(0, 4 - j)
            if nh > 0:
                P.op(ACT, lambda e, kb=kb, nh=nh: e.activation(PT[kb][:, 0:nh, :], SS[kb][:, 0:nh, :], AF.Exp,
                                                             bias=pvt[:, 1:2]),
                     reads=["SS%d" % kb, "pv"], writes=["PT%d" % kb])
            P.op(ACT, lambda e, kb=kb, nh=nh: e.activation(PT[kb][:, nh:5, :], SS[kb][:, nh:5, :], AF.Exp),
                 reads=["SS%d" % kb], writes=["PT%d" % kb])

        def st_C(i):
            hp, j, hh = i // 32, (i % 32) // 2, i % 2
            b = hp % 2
            kb = i % 3
            k2 = i % 2
            for kt in range(5):
                P.op(PE, lambda e, kb=kb, k2=k2, kt=kt, j=j, hp=hp, hh=hh: e.matmul(
                    Ob[k2], PT[kb][:, kt, :], V[:, j + kt, hp * 2 + hh, :],
                    start=(kt == 0), stop=(kt == 4)),
                    reads=["PT%d" % kb, "V%d_%d" % (j + kt, hp // 4), "Vones"], writes=["po%d" % k2])
            P.op(DVE, lambda e, kb=kb, k2=k2: e.reciprocal(rsum[:, kb:kb + 1], Ob[k2][:, 64:65]),
                 reads=["po%d" % k2], writes=["rs_%d" % kb])
            P.op(ACT, lambda e, kb=kb, k2=k2, hh=hh, b=b, j=j: e.activation(
                AH[b][:, j, hh * 64:(hh + 1) * 64], Ob[k2][:, 0:64], AF.Copy, scale=rsum[:, kb:kb + 1]),
                reads=["po%d" % k2, "rs_%d" % kb], writes=["AH%d" % b])

        def st_T(hp):
            b = hp % 2
            for g in range(2):
                ps, pr = next_pg()
                pv3 = pg_bf(ps)
                for i in range(8):
                    P.op(PE, lambda e, pv3=pv3, i=i, g=g, b=b: e.transpose(pv3[:, i, :], AH[b][:, g * 8 + i, :], identb[:]),
                         reads=["AH%d" % b, "ident"], writes=[pr])
                P.op(ACT, lambda e, pv3=pv3, g=g, hp=hp: e.copy(
                    AT[:, hp, g * 1024:(g + 1) * 1024].rearrange("p (a b) -> p a b", a=8), pv3),
                    reads=[pr], writes=["AT%d_%d" % (hp, g)])

        NIT = 8 * 16 * 2
        pg_avail[0] = [6, 7]
        load_w(0)
        for n in range(9):
            proj_chunk(0, n)
        st_S(0)
        st_B(0)
        st_S(1)
        st_B(1)
        for i in range(NIT):
            hp, r = divmod(i, 32)
            if r == 0 and hp + 1 < 8:
                load_w(hp + 1)
            if r % 2 == 0 and 4 <= r < 22 and hp + 1 < 8:
                proj_chunk(hp + 1, (r - 4) // 2)
            if r == 2 and hp >= 1:
                st_T(hp - 1)
            if i + 2 < NIT:
                st_S(i + 2)
                st_B(i + 2)
            st_C(i)
        branch_pre(w_ao, 5 * D, 172 * KB)
        st_T(7)
        pg_avail[0] = list(range(8))
        P.barrier()

        GT = view(40 * KB, [128, 8, 2048], BF16)

        def branch(wsrc, gate_c0, src3, bcol, first, WOFF, TOFF, preloaded=False, hook=None, alt0=None):
            WA = [view(WOFF + i * 2048, [128, 8, 128], BF16) for i in range(2)]
            WG = [view(WOFF + 4096 + i * 2048, [128, 8, 128], BF16) for i in range(2)]
            sgb = [view(TOFF + i * 2048, [128, 512], F32) for i in range(2)]
            tmb = [view(TOFF + 4096 + i * 2048, [128, 512], F32) for i in range(2)]

            def loadw(oc):
                wb_ = oc % 2 if alt0 is None else 1 + (oc % 2)
                P.dma(POOL, WA[wb_], wslice(wsrc, oc * 128, 128), writes=["WA%d" % wb_])
                P.dma(POOL, WG[wb_], wslice(w_in, gate_c0 + oc * 128, 128), writes=["WG%d" % wb_])
            if not preloaded:
                loadw(0)
            if alt0 is not None:
                WA = [view(alt0, [128, 8, 128], BF16)] + WA
                WG = [view(alt0 + 2048, [128, 8, 128], BF16)] + WG
            it = 0
            for oc in range(8):
                wb_ = oc % 2
                if oc + 1 < 8:
                    loadw(oc + 1)
                if alt0 is not None:
                    wb_ = 0 if oc == 0 else 1 + (oc % 2)
                if oc == 0 and hook is not None:
                    hook()
                for n in range(4):
                    cs = slice(n * 512, (n + 1) * 512)
                    b = it % 2
                    it += 1
                    psG, prG = next_pg()
                    for k in range(8):
                        P.op(PE, lambda e, psG=psG, k=k, wb_=wb_, n=n: e.matmul(psG, WG[wb_][:, k, :],
                                                                            XT[:, k, 512 + n * 512:512 + (n + 1) * 512],
                                                                            start=(k == 0), stop=(k == 7)),
                             reads=["WG%d" % wb_], writes=[prG])
                    psY, prY = next_pg()
                    for k in range(8):
                        P.op(PE, lambda e, psY=psY, k=k, wb_=wb_, cs=cs: e.matmul(psY, WA[wb_][:, k, :],
                                                                              src3[:, k, cs], start=(k == 0), stop=(k == 7)),
                             reads=["WA%d" % wb_, "SRC"], writes=[prY])
                    P.op(ACT, lambda e, psG=psG, b=b, oc=oc: e.activation(sgb[b], psG, AF.Sigmoid,
                                                                      bias=cv[:, bcol + oc:bcol + oc + 1]),
                         reads=[prG, "cv"], writes=["sg%d" % b])
                    if first:
                        P.op(DVE, lambda e, psY=psY, b=b, oc=oc, cs=cs: e.tensor_tensor(GT[:, oc, cs], sgb[b], psY, ALU.mult),
                             reads=[prY, "sg%d" % b], writes=["GT"])
                    else:
                        P.op(DVE, lambda e, psY=psY, b=b: e.tensor_tensor(tmb[b], sgb[b], psY, ALU.mult),
                             reads=[prY, "sg%d" % b], writes=["tm%d" % b])
                        P.op(POOL, lambda e, b=b, oc=oc, cs=cs: e.tensor_tensor(GT[:, oc, cs], tmb[b], GT[:, oc, cs], ALU.add),
                             reads=["tm%d" % b], writes=["GT"])

        branch(w_ao, 5 * D, AT, BMA, True, 172 * KB, 116 * KB, preloaded=True)
        P.dma(POOL, view(180 * KB, [128, 8, 128], BF16), wslice(w_in, 4 * D, 128), writes=["wg0"])
        P.barrier()
        gelu_phase(176 * KB, preloaded=True)
        BT = rnn_pass(True, 72 * KB, 176 * KB)
        P.dma(POOL, view(184 * KB, [128, 8, 128], BF16), wslice(w_ro, 0, 128), writes=["WA0"])
        P.dma(POOL, view(184 * KB + 2048, [128, 8, 128], BF16), wslice(w_in, 6 * D, 128), writes=["WG0"])
        P.barrier()
        WO = view(88 * KB, [128, 8, 1024], BF16)
        branch(w_ro, 6 * D, BT, BMB, False, 72 * KB, 80 * KB, preloaded=True, alt0=184 * KB,
               hook=lambda: P.dma(POOL, WO, wslice(w_o, 0, D), writes=["WO"]))
        xt2 = [view(104 * KB + i * 4096, [128, D], F32) for i in range(2)]
        for t in range(2):
            P.dma(SP, xt2[t], xm[t * 128:(t + 1) * 128, :], writes=["x2_%d" % t])
        P.barrier()

        H = view(120 * KB, [128, 16, D], F32)
        xt2 = [view(104 * KB + i * 4096, [128, D], F32) for i in range(2)]
        HT = view(0, [128, 8, 2048], BF16)
        g2 = view(72 * KB, [128, D], F32)
        g3 = view(100 * KB, [128, D], F32)
        xn2 = [view(112 * KB + i * 2048, [128, D], BF16) for i in range(3)]
        junkW = view(118 * KB, [128, D], BF16)
        P.dma(SP, g2, gains[1:2, :].broadcast_to([128, D]), writes=["g"])
        P.op(DVE, lambda e: e.memset(ss, 0.0), writes=["ss%d" % i for i in range(32)])
        for t in range(16):
            b = t % 2
            if t >= 2:
                P.dma(SP, xt2[b], xm[t * 128:(t + 1) * 128, :], writes=["x2_%d" % b])
            for cc in range(2):
                ps, pr = next_pg()
                for k in range(8):
                    P.op(PE, lambda e, ps=ps, k=k, t=t, cc=cc: e.matmul(ps, GT[:, k, t * 128:(t + 1) * 128],
                                                                    WO[:, k, cc * 512:(cc + 1) * 512],
                                                                    start=(k == 0), stop=(k == 7)),
                         reads=["WO"], writes=[pr])
                P.op(DVE, lambda e, ps=ps, t=t, cc=cc, b=b: e.tensor_tensor(H[:, t, cc * 512:(cc + 1) * 512], ps,
                                                                        xt2[b][:, cc * 512:(cc + 1) * 512], ALU.add),
                     reads=[pr, "x2_%d" % b], writes=["H%d" % t])
            norm_s(H[:, t, :], "H%d" % t, t, junkW)
            if t >= 1:
                norm_m(H[:, t - 1, :], "H%d" % (t - 1), g2, t - 1, xn2[(t - 1) % 3], "xn%d" % ((t - 1) % 3))
            if t >= 2:
                norm_b(HT[:, :, (t - 2) * 128:(t - 1) * 128], "HT%d" % (t - 2), xn2[(t - 2) % 3], "xn%d" % ((t - 2) % 3))
        norm_m(H[:, 15, :], "H15", g2, 15, xn2[15 % 3], "xn%d" % (15 % 3))
        norm_b(HT[:, :, 14 * 128:15 * 128], "HT14", xn2[14 % 3], "xn%d" % (14 % 3))
        norm_b(HT[:, :, 15 * 128:16 * 128], "HT15", xn2[15 % 3], "xn%d" % (15 % 3))
        P.dma(POOL, view(32 * KB, [128, 8, 128], BF16), wslice(w_fi, 0, 128), writes=["WIg0"])
        P.dma(POOL, view(32 * KB + 2048, [128, 8, 128], BF16), wslice(w_fi, DFF, 128), writes=["WIu0"])
        P.barrier()

        WI = [view(32 * KB + i * 2048, [128, 8, 128], BF16) for i in range(4)]
        FT = view(40 * KB, [128, 8, 2048], BF16)
        WOUT = view(72 * KB, [128, 8, 1024], BF16)
        slb = [view(88 * KB + i * 2048, [128, 512], F32) for i in range(2)]
        ob = [view(92 * KB + i * 4096, [128, D], F32) for i in range(2)]
        P.dma(SP, g3, gains[2:3, :].broadcast_to([128, D]), writes=["g3"])
        stores = []
        junk3 = xn2[0]

        def final_s(t):
            si = 16 + t
            P.op(ACT, lambda e, t=t, si=si: e.activation(junk3, H[:, t, :], AF.Square, accum_out=ss[:, si:si + 1]),
                 reads=["H%d" % t], writes=["ss%d" % si])
            P.op(ACT, lambda e, si=si: e.activation(rs[:, si:si + 1], ss[:, si:si + 1], AF.Sqrt,
                                                   scale=1.0 / D, bias=cst[:, 1:2]),
                 reads=["ss%d" % si, "cst"], writes=["rs%d" % si])

        def final_m(t):
            si = 16 + t
            b = t % 2
            P.op(DVE, lambda e, si=si: e.reciprocal(rs[:, si:si + 1], rs[:, si:si + 1]),
                 reads=["rs%d" % si], writes=["rs%d" % si])
            P.op(DVE, lambda e, t=t, si=si, b=b: e.scalar_tensor_tensor(ob[b], H[:, t, :], rs[:, si:si + 1], g3,
                                                                   ALU.mult, ALU.mult),
                 reads=["H%d" % t, "rs%d" % si, "g3"], writes=["ob%d" % b])
            stores.append(P.dma(SP, out[t * 128:(t + 1) * 128, :], ob[b], reads=["ob%d" % b]))

        fi = 0
        for (f0, f1) in FGROUPS:
            nf = f1 - f0
            for f in range(f0, f1):
                fl = f - f0
                b = fi % 2
                fi += 1
                if f > 0:
                    P.dma(POOL, WI[2 * b], wslice(w_fi, f * 128, 128), writes=["WIg%d" % b])
                    P.dma(POOL, WI[2 * b + 1], wslice(w_fi, DFF + f * 128, 128), writes=["WIu%d" % b])
                P.dma(POOL, WOUT[:, fl, :], w_fo[f * 128:(f + 1) * 128, :], writes=["WOUT%d" % fl])
                for n in range(4):
                    cs = slice(n * 512, (n + 1) * 512)
                    psG, prG = next_pg()
                    for k in range(8):
                        P.op(PE, lambda e, psG=psG, k=k, cs=cs, b=b: e.matmul(psG[:], WI[2 * b][:, k, :], HT[:, k, cs],
                                                                         start=(k == 0), stop=(k == 7)),
                             reads=["WIg%d" % b, "HT"], writes=[prG])
                    psU, prU = next_pg()
                    for k in range(8):
                        P.op(PE, lambda e, psU=psU, k=k, cs=cs, b=b: e.matmul(psU[:], WI[2 * b + 1][:, k, :], HT[:, k, cs],
                                                                         start=(k == 0), stop=(k == 7)),
                             reads=["WIu%d" % b, "HT"], writes=[prU])
                    sb2 = n % 2
                    P.op(ACT, lambda e, psG=psG, sb2=sb2: e.activation(slb[sb2], psG[:], AF.Silu),
                         reads=[prG], writes=["sl%d" % sb2])
                    P.op(DVE, lambda e, psU=psU, sb2=sb2, fl=fl, cs=cs: e.tensor_tensor(FT[:, fl, cs], slb[sb2], psU[:], ALU.mult),
                         reads=[prU, "sl%d" % sb2], writes=["FT"])
            for t in range(16):
                for cc in range(2):
                    ps, pr = next_pg()
                    for fl in range(nf):
                        P.op(PE, lambda e, ps=ps, fl=fl, t=t, cc=cc, nf=nf: e.matmul(
                            ps[:], FT[:, fl, t * 128:(t + 1) * 128], WOUT[:, fl, cc * 512:(cc + 1) * 512],
                            start=(fl == 0), stop=(fl == nf - 1)),
                            reads=["FT", "WOUT%d" % fl], writes=[pr])
                    P.op(DVE, lambda e, ps=ps, t=t, cc=cc: e.tensor_tensor(H[:, t, cc * 512:(cc + 1) * 512], ps[:],
                                                                       H[:, t, cc * 512:(cc + 1) * 512], ALU.add),
                         reads=[pr], writes=["H%d" % t])
                if f1 == FGROUPS[-1][1]:
                    final_s(t)
                    if t >= 1:
                        final_m(t - 1)
        final_m(15)
        P.emit(final_waits=stores)
        build_program.stats = P.stats
    return nc


def _chan(v):
    return np.ascontiguousarray(np.asarray(v, np.float32).reshape(8, 128).T)


def _bias_tiles(rel_table):
    tab = np.concatenate([np.asarray(rel_table, np.float32), np.full((NH, 1), NEG, np.float32)], axis=1)
    m = np.arange(640)[:, None]
    i = np.arange(128)[None, :]
    dist = i + 512 - m
    idx = np.clip(dist, -128, 128) + 128
    kc = m // 64
    qc = i // 64
    ok = (kc >= qc) & (kc <= qc + 8)
    idx = np.where(ok, idx, 257)
    full = tab[:, idx]
    full = full.reshape(8, 2, 5, 128, 128).transpose(0, 3, 1, 2, 4)
    return np.ascontiguousarray(full).reshape(8, 128, 2 * 5 * 128)


def _blockdiag(w):
    w = np.asarray(w, np.float32)
    o = np.zeros((128, 8, 128), np.float32)
    for c in range(8):
        o[0:64, c, 0:64] = w[2 * c]
        o[64:128, c, 64:128] = w[2 * c + 1]
    return o


_NC_CACHE = {}


def kernel(x, norm_mix_g, w_in, b_merge, rel_table, w_attn_out, conv_w, conv_b,
           w_rg_r, b_rg_r, w_rg_i, b_rg_i, rg_lambda, w_rnn_out, w_o,
           norm_ffn_g, w_ffn_in, w_ffn_out, final_norm_g):
    x = np.asarray(x, np.float32)
    f = lambda a: np.ascontiguousarray(np.asarray(a, np.float32))
    cvec = np.zeros((128, NV), np.float32)
    cw = np.asarray(conv_w, np.float32)[0]
    for j in range(4):
        cvec[:, CW + 8 * j: CW + 8 * j + 8] = _chan(cw[j])
    cvec[:, CB:CB + 8] = _chan(np.asarray(conv_b)[0])
    cvec[:, BR:BR + 8] = _chan(np.asarray(b_rg_r)[0])
    cvec[:, BI:BI + 8] = _chan(np.asarray(b_rg_i)[0])
    cvec[:, LAM:LAM + 8] = _chan(np.asarray(rg_lambda)[0])
    bm = np.asarray(b_merge, np.float32)[0]
    cvec[:, BMA:BMA + 8] = _chan(bm[:D])
    cvec[:, BMB:BMB + 8] = _chan(bm[D:])
    gains = np.stack([np.asarray(norm_mix_g, np.float32)[0], np.asarray(norm_ffn_g, np.float32)[0],
                      np.asarray(final_norm_g, np.float32)], 0)
    shared = dict(
        w_in=f(np.asarray(w_in)[0]), w_ao=f(np.asarray(w_attn_out)[0]), w_ro=f(np.asarray(w_rnn_out)[0]),
        w_o=f(np.asarray(w_o)[0]), w_fi=f(np.asarray(w_ffn_in)[0]), w_fo=f(np.asarray(w_ffn_out)[0]),
        wr_bd=_blockdiag(np.asarray(w_rg_r)[0]), wi_bd=_blockdiag(np.asarray(w_rg_i)[0]),
        cvec=cvec, gains=np.ascontiguousarray(gains), biasT=_bias_tiles(np.asarray(rel_table)[0]),
        cbrow=f(np.asarray(conv_b)[0].reshape(1, D)),
    )
    in_maps = []
    for c in range(8):
        b, hf = c // 2, c % 2
        m = dict(shared)
        m["xm"] = np.ascontiguousarray(x[b, hf * TM:(hf + 1) * TM])
        pvv = np.zeros((128, 2), np.float32)
        if hf == 1:
            m["xp"] = np.ascontiguousarray(x[b, 0:TPRE])
            pvv[:, 0] = 1.0
        else:
            m["xp"] = np.zeros((TPRE, D), np.float32)
            pvv[:, 1] = NEG
        m["pv"] = pvv
        in_maps.append(m)
    if "nc" not in _NC_CACHE:
        _NC_CACHE["nc"] = build_program()
    nc = _NC_CACHE["nc"]
    res = run_bass_kernel_spmd(nc, in_maps, core_ids=list(range(8)))
    outp = np.zeros((4, 4096, D), np.float32)
    for c in range(8):
        b, hf = c // 2, c % 2
        outp[b, hf * TM:(hf + 1) * TM] = res.results[c]["out"]
    return outp


if __name__ == "__main__":
    import time
    t0 = time.time()
    nc = build_program()
    print("build ok", time.time() - t0, build_program.stats)
```
